# Optimizing a Trainium2 kernel written in Bass

```python
import jax, jax.numpy as jnp
from jax import lax
import numpy as np

D_MODEL = 1024
BATCH = 4
SEQ = 8192
DEPTH = 1

M_HEADS = 4
M_HEAD_DIM = 128
M_WIDTH = M_HEADS * M_HEAD_DIM
CONV_W = 4
CHUNK = 128
A_HEADS = 8
A_HEAD_DIM = 64
A_WIDTH = A_HEADS * A_HEAD_DIM
DIL_PATTERNS = ((128, 1), (512, 4), (2048, 16))
N_BUCKETS = 32
MAX_DISTANCE = 2048
MIX_WIDTH = M_WIDTH + A_WIDTH
IN_SIZES = (M_WIDTH, M_WIDTH, M_WIDTH, M_HEADS, M_HEADS, A_WIDTH, A_WIDTH, A_WIDTH)
IN_COLS = 3 * M_WIDTH + 2 * M_HEADS + 3 * A_WIDTH
PEER_HEADS = 8
N_KEYS = 128
N_EXPERTS = N_KEYS * N_KEYS
PEER_QDIM = 256
PEER_TOPK = 16
PEER_BLOCK = 128
EPS = 1e-6

kernel_name = 'hymba_mlstm_dilattn_peer_layer'


def rms_norm(x, g):
    xf = x.astype(jnp.float32)
    y = xf * lax.rsqrt(jnp.mean(xf * xf, axis=-1, keepdims=True) + EPS)
    return (y * g).astype(x.dtype)


def t5_bucket(dist):
    max_exact = N_BUCKETS // 2
    nf = jnp.maximum(dist, 1).astype(jnp.float32)
    large = max_exact + (jnp.log(nf / max_exact) / np.log(MAX_DISTANCE / max_exact)
                         * (N_BUCKETS - max_exact)).astype(jnp.int32)
    large = jnp.minimum(large, N_BUCKETS - 1)
    return jnp.where(dist < max_exact, dist, large)


def causal_short_conv(u, w, b):
    y = lax.conv_general_dilated(u, w[:, None, :], window_strides=(1,),
                                 padding=[(CONV_W - 1, 0)],
                                 dimension_numbers=('NWC', 'WIO', 'NWC'),
                                 feature_group_count=u.shape[-1])
    return y + b


def mlstm_chunkwise(q, k, v, ipre, fpre):
    B, H, S, Dh = q.shape
    nc = S // CHUNK
    q = q.reshape(B, H, nc, CHUNK, Dh)
    k = k.reshape(B, H, nc, CHUNK, Dh)
    v = v.reshape(B, H, nc, CHUNK, Dh)
    logi = ipre.reshape(B, H, nc, CHUNK)
    b = jnp.cumsum(jax.nn.log_sigmoid(fpre).reshape(B, H, nc, CHUNK), axis=-1)
    b_last = b[..., -1]
    g = b_last[..., None] - b + logi
    m_loc = jnp.max(g, axis=-1)
    wts = jnp.exp(g - m_loc[..., None])
    c_chunk = jnp.einsum('bhnl,bhnlk,bhnlv->bhnkv', wts, k, v)
    n_chunk = jnp.einsum('bhnl,bhnlk->bhnk', wts, k)

    def step(carry, inp):
        C, n, m = carry
        bl, ml, cc, nn = inp
        m_new = jnp.maximum(bl + m, ml)
        a = jnp.exp(bl + m - m_new)
        bb = jnp.exp(ml - m_new)
        C_new = a[..., None, None] * C + bb[..., None, None] * cc
        n_new = a[..., None] * n + bb[..., None] * nn
        return (C_new, n_new, m_new), (C, n, m)

    init = (jnp.zeros((B, H, Dh, Dh), jnp.float32), jnp.zeros((B, H, Dh), jnp.float32),
            jnp.zeros((B, H), jnp.float32))
    xs = (jnp.moveaxis(b_last, 2, 0), jnp.moveaxis(m_loc, 2, 0),
          jnp.moveaxis(c_chunk, 2, 0), jnp.moveaxis(n_chunk, 2, 0))
    _, (C_prev, n_prev, m_prev) = lax.scan(step, init, xs)
    C_prev = jnp.moveaxis(C_prev, 0, 2)
    n_prev = jnp.moveaxis(n_prev, 0, 2)
    m_prev = jnp.moveaxis(m_prev, 0, 2)

    causal = jnp.tril(jnp.ones((CHUNK, CHUNK), bool))
    log_d = jnp.where(causal, b[..., :, None] - b[..., None, :] + logi[..., None, :], -jnp.inf)
    inter = b + m_prev[..., None]
    m_t = jnp.maximum(inter, jnp.max(log_d, axis=-1))
    d_mat = jnp.exp(log_d - m_t[..., None])
    inter_w = jnp.exp(inter - m_t)
    s = jnp.einsum('bhnqd,bhnkd->bhnqk', q, k) * d_mat
    num = (jnp.einsum('bhnqk,bhnkd->bhnqd', s, v)
           + inter_w[..., None] * jnp.einsum('bhnqd,bhnde->bhnqe', q, C_prev))
    den = jnp.sum(s, axis=-1) + inter_w * jnp.einsum('bhnqd,bhnd->bhnq', q, n_prev)
    h = num / jnp.maximum(jnp.abs(den), jnp.exp(-m_t))[..., None]
    return h.reshape(B, H, S, Dh)


def dilated_branch(q, k, v, rel_bias, window, dilation):
    B, S, H, Dh = q.shape
    blk = window // dilation
    span = dilation * blk
    s_pad = -(-S // span) * span
    nb = s_pad // span

    def to_blocks(t):
        t = jnp.pad(t, ((0, 0), (0, s_pad - S), (0, 0), (0, 0)))
        return t.reshape(B, nb, blk, dilation, H, Dh).transpose(0, 3, 4, 1, 2, 5)

    qb, kb, vb = to_blocks(q), to_blocks(k), to_blocks(v)
    shift = lambda t: jnp.pad(t, ((0, 0), (0, 0), (0, 0), (1, 0), (0, 0), (0, 0)))[:, :, :, :-1]
    kk = jnp.concatenate([shift(kb), kb], axis=4)
    vv = jnp.concatenate([shift(vb), vb], axis=4)
    i = jnp.arange(blk)[:, None]
    j = jnp.arange(2 * blk)[None, :]
    steps = i + blk - j
    band = (steps >= 0) & (steps <= blk)
    valid = band[None] & ((jnp.arange(nb)[:, None, None] > 0) | (j >= blk)[None])
    bias = rel_bias[t5_bucket(jnp.maximum(steps, 0) * dilation)].transpose(2, 0, 1)
    s = (jnp.einsum('brhnqc,brhnkc->brhnqk', qb, kk).astype(jnp.float32) * (A_HEAD_DIM ** -0.5)
         + bias[None, None, :, None])
    s = jnp.where(valid[None, None, None], s, -jnp.inf)
    m = jnp.max(s, axis=-1, keepdims=True)
    p = jnp.exp(s - m)
    den = jnp.sum(p, axis=-1)
    o = jnp.einsum('brhnqk,brhnkc->brhnqc', p, vv) / den[..., None]
    lse = m[..., 0] + jnp.log(den)
    o = o.transpose(0, 3, 4, 1, 2, 5).reshape(B, s_pad, H, Dh)[:, :S]
    lse = lse.transpose(0, 3, 4, 1, 2).reshape(B, s_pad, H)[:, :S]
    return o, lse


def peer(h, w_query, sub_keys1, sub_keys2, expert_u, expert_v):
    B, S, D = h.shape
    T = B * S
    hf = h.reshape(T, D)
    q = (hf @ w_query).reshape(T, PEER_HEADS, 2, PEER_QDIM // 2)
    s1 = jnp.einsum('thc,hkc->thk', q[:, :, 0], sub_keys1).astype(jnp.float32)
    s2 = jnp.einsum('thc,hkc->thk', q[:, :, 1], sub_keys2).astype(jnp.float32)
    v1, i1 = lax.top_k(s1, PEER_TOPK)
    v2, i2 = lax.top_k(s2, PEER_TOPK)
    cand = (v1[..., :, None] + v2[..., None, :]).reshape(T, PEER_HEADS, PEER_TOPK * PEER_TOPK)
    vc, ic = lax.top_k(cand, PEER_TOPK)
    e1 = jnp.take_along_axis(i1, ic // PEER_TOPK, axis=-1)
    e2 = jnp.take_along_axis(i2, ic % PEER_TOPK, axis=-1)
    idx = (e1 * N_KEYS + e2).reshape(T, PEER_HEADS * PEER_TOPK)
    gate = jax.nn.softmax(vc, axis=-1).reshape(T, PEER_HEADS * PEER_TOPK)

    def block(args):
        xb, ib, gb = args
        u = expert_u[ib]
        pre = jnp.einsum('td,tkd->tk', xb, u).astype(jnp.float32)
        a = jax.nn.gelu(pre, approximate=False) * gb
        return jnp.einsum('tk,tkd->td', a, expert_v[ib])

    nblk = T // PEER_BLOCK
    out = lax.map(block, (hf.reshape(nblk, PEER_BLOCK, D),
                          idx.reshape(nblk, PEER_BLOCK, -1),
                          gate.reshape(nblk, PEER_BLOCK, -1)))
    return out.reshape(B, S, D).astype(h.dtype)


def setup_inputs(seed: int = 0) -> dict:
    key = jax.random.key(seed)
    ks = jax.random.split(key, 24)
    n = lambda i, shape: jax.random.normal(ks[i], shape, jnp.float32)
    return {
        'x': n(0, (BATCH, SEQ, D_MODEL)),
        'norm1_g': 1.0 + 0.02 * n(1, (DEPTH, D_MODEL)),
        'w_in': n(2, (DEPTH, D_MODEL, IN_COLS)) * D_MODEL ** -0.5,
        'conv_w': n(3, (DEPTH, CONV_W, M_WIDTH)) * CONV_W ** -0.5,
        'conv_b': 0.02 * n(4, (DEPTH, M_WIDTH)),
        'wq_m': n(5, (DEPTH, M_HEADS, M_HEAD_DIM, M_HEAD_DIM)) * M_HEAD_DIM ** -0.5,
        'wk_m': n(6, (DEPTH, M_HEADS, M_HEAD_DIM, M_HEAD_DIM)) * M_HEAD_DIM ** -0.5,
        'ig_b': 0.1 * n(7, (DEPTH, M_HEADS)),
        'fg_b': jnp.linspace(3.0, 6.0, M_HEADS, dtype=jnp.float32)[None] + 0.1 * n(8, (DEPTH, M_HEADS)),
        'mh_norm_g': 1.0 + 0.02 * n(9, (DEPTH, M_HEADS, M_HEAD_DIM)),
        'skip_m': 1.0 + 0.02 * n(10, (DEPTH, M_HEADS, M_HEAD_DIM)),
        'qn_g': 1.0 + 0.02 * n(11, (DEPTH, A_HEADS, A_HEAD_DIM)),
        'kn_g': 1.0 + 0.02 * n(12, (DEPTH, A_HEADS, A_HEAD_DIM)),
        'rel_bias': 0.5 * n(13, (N_BUCKETS, A_HEADS)),
        'w_out': n(14, (DEPTH, MIX_WIDTH, D_MODEL)) * MIX_WIDTH ** -0.5,
        'norm2_g': 1.0 + 0.02 * n(15, (DEPTH, D_MODEL)),
        'w_query': n(16, (DEPTH, D_MODEL, PEER_HEADS * PEER_QDIM)) * D_MODEL ** -0.5,
        'sub_keys1': n(17, (DEPTH, PEER_HEADS, N_KEYS, PEER_QDIM // 2)) * (PEER_QDIM // 2) ** -0.5,
        'sub_keys2': n(18, (DEPTH, PEER_HEADS, N_KEYS, PEER_QDIM // 2)) * (PEER_QDIM // 2) ** -0.5,
        'expert_u': n(19, (DEPTH, N_EXPERTS, D_MODEL)) * D_MODEL ** -0.5,
        'expert_v': n(20, (DEPTH, N_EXPERTS, D_MODEL)) * D_MODEL ** -0.5,
    }


def reference(x, norm1_g, w_in, conv_w, conv_b, wq_m, wk_m, ig_b, fg_b, mh_norm_g, skip_m,
              qn_g, kn_g, rel_bias, w_out, norm2_g, w_query, sub_keys1, sub_keys2,
              expert_u, expert_v):
    B, S, D = x.shape
    offsets = []
    acc = 0
    for sz in IN_SIZES[:-1]:
        acc += sz
        offsets.append(acc)
    for l in range(DEPTH):
        h = rms_norm(x, norm1_g[l])
        proj = h @ w_in[l]
        u, vm, z, ipre, fpre, qa, ka, va = jnp.split(proj, offsets, axis=-1)
        c = jax.nn.silu(causal_short_conv(u, conv_w[l], conv_b[l]))
        ch = c.reshape(B, S, M_HEADS, M_HEAD_DIM)
        qm = jnp.einsum('bshc,hcd->bhsd', ch, wq_m[l]).astype(jnp.float32)
        km = jnp.einsum('bshc,hcd->bhsd', ch, wk_m[l]).astype(jnp.float32) * (M_HEAD_DIM ** -0.5)
        vmh = vm.reshape(B, S, M_HEADS, M_HEAD_DIM).transpose(0, 2, 1, 3).astype(jnp.float32)
        ig = (ipre + ig_b[l]).astype(jnp.float32).transpose(0, 2, 1)
        fg = (fpre + fg_b[l]).astype(jnp.float32).transpose(0, 2, 1)
        hm = mlstm_chunkwise(qm, km, vmh, ig, fg).transpose(0, 2, 1, 3).astype(x.dtype)
        hm = rms_norm(hm, mh_norm_g[l]) + skip_m[l] * ch
        ym = jax.nn.sigmoid(z) * hm.reshape(B, S, M_WIDTH)
        qh = rms_norm(qa.reshape(B, S, A_HEADS, A_HEAD_DIM), qn_g[l])
        kh = rms_norm(ka.reshape(B, S, A_HEADS, A_HEAD_DIM), kn_g[l])
        vh = va.reshape(B, S, A_HEADS, A_HEAD_DIM)
        outs, lses = [], []
        for window, dilation in DIL_PATTERNS:
            o, lse = dilated_branch(qh, kh, vh, rel_bias, window, dilation)
            outs.append(o)
            lses.append(lse)
        wts = jax.nn.softmax(jnp.stack(lses, axis=0), axis=0)
        ya = jnp.sum(wts[..., None] * jnp.stack(outs, axis=0), axis=0)
        ya = ya.reshape(B, S, A_WIDTH).astype(x.dtype)
        x = x + jnp.concatenate([ym, ya], axis=-1) @ w_out[l]
        h2 = rms_norm(x, norm2_g[l])
        x = x + peer(h2, w_query[l], sub_keys1[l], sub_keys2[l], expert_u[l], expert_v[l])
    return x
```

```python
import numpy as np
from contextlib import ExitStack
import concourse.bass as bass
import concourse.mybir as mybir
from concourse.bass_utils import run_bass_kernel_spmd

F32 = mybir.dt.float32
BF16 = mybir.dt.bfloat16
I32 = mybir.dt.int32
U32 = mybir.dt.uint32
AF = mybir.ActivationFunctionType
ALU = mybir.AluOpType
AX = mybir.AxisListType

D = 1024
INC = 3080
EPS = 1e-6
NEG = -30000.0
PATS = ((128, 1), (512, 4), (2048, 16))
W_ROT = 30000
STRICT = True
ENGS = ['pe', 'act', 'dve', 'pool', 'sp']


class Prog:
    def __init__(self, nc, name):
        self.nc = nc
        self.name = name
        self.ops = {e: [] for e in ENGS}
        self.cnt = {e: 0 for e in ENGS}
        self.lastw = {}
        self.readers = {}
        self.waited = {e: {} for e in ENGS}
        self.scnt = {}
        self.final_streams = set()

    def add(self, e, fn, r=(), w=(), stream=None, final=False):
        deps = {}
        def dep(tok):
            if tok is None:
                return
            sid, v = tok
            if v is None:
                deps[sid] = None
            elif sid not in deps or (deps[sid] is not None and deps[sid] < v):
                deps[sid] = v
        def same_eng(tok):
            if STRICT:
                return False
            return tok is not None and tok[0][0] == 'E' and tok[0][1] == e
        for k in r:
            dep(self.lastw.get(k))
        for k in w:
            t0 = self.lastw.get(k)
            if not same_eng(t0):
                dep(t0)
            for t in self.readers.get(k, ()):
                if not same_eng(t):
                    dep(t)
        if stream is None:
            idx = self.cnt[e]
            self.cnt[e] += 1
            j = idx // W_ROT
            tok = (('E', e, j), idx - j * W_ROT + 1)
            inc = (tok[0], 1)
        else:
            n = self.scnt.get(stream, 0) + 1
            self.scnt[stream] = n
            if final:
                self.final_streams.add(stream)
                tok = (('D', stream), None)
            else:
                tok = (('D', stream), 16 * n)
            inc = (('D', stream), 16)
        waits = []
        for sid, v in deps.items():
            if e == 'pe' and sid[0] == 'E' and sid[1] == 'pe':
                continue
            if v is None:
                if self.waited[e].get(sid) == 'F':
                    continue
                self.waited[e][sid] = 'F'
                waits.append((sid, None))
                continue
            pv = self.waited[e].get(sid, 0)
            if pv == 'F' or pv >= v:
                continue
            self.waited[e][sid] = v
            waits.append((sid, v))
        self.ops[e].append((waits, fn, inc))
        for k in w:
            self.lastw[k] = tok
            self.readers[k] = []
        for k in r:
            if k not in w:
                self.readers.setdefault(k, []).append(tok)
        return tok

    def emit(self, es, final_wait_engine='sp'):
        nc = self.nc
        sems = {}

        def get(sid):
            if sid not in sems:
                nm = self.name + '_' + '_'.join(str(s) for s in sid)
                sems[sid] = es.enter_context(nc.semaphore(nm.replace(':', '_').replace('-', 'm')))
            return sems[sid]

        fin = []
        for st, n in self.scnt.items():
            fin.append((('D', st), 16 * n))
        self.ops[final_wait_engine].append((fin, None, None))
        with nc.Block() as block:
            decos = {'pe': block.tensor, 'act': block.scalar, 'dve': block.vector,
                     'pool': block.gpsimd, 'sp': block.sync}
            for e in ENGS:
                ops = self.ops[e]

                def body(eng, ops=ops):
                    for waits, fn, inc in ops:
                        for sid, v in waits:
                            if v is None:
                                v = 16 * self.scnt[sid[1]]
                            eng.wait_ge(get(sid), v)
                        if fn is None:
                            continue
                        ins = fn(eng)
                        ins.then_inc(get(inc[0]), inc[1])
                decos[e](body)
        return len(sems)


def build(NPRE, NOWN, with_peer=True):
    NT = NPRE + NOWN
    NHALO = 16
    assert NPRE >= NHALO and NPRE % 16 == 0 and NOWN % 16 == 0
    T0H = NPRE - NHALO
    nc = bass.Bass("TRN2", target_bir_lowering=False)

    def din(name, shape, dt=F32):
        return nc.dram_tensor(name, list(shape), dt, kind="ExternalInput").ap()

    xall = din("xall", [NT * 128, D])
    flg = din("flg", [128, 2])
    w_in = din("w_in", [D, INC])
    g1c = din("g1c", [128, 8])
    cwd = din("cw", [128, 16])
    cbd = din("cb", [128, 4])
    wqm = din("wqm", [128, 4, 128])
    wkm = din("wkm", [128, 4, 128])
    gbd = din("gb", [1, 8])
    mgd = din("mg", [128, 4])
    skd = din("sk", [128, 4])
    qgd = din("qg", [128, 4])
    kgd = din("kg", [128, 4])
    tbd = din("tb", [3, 128, 8 * 256])
    w_out = din("w_out", [D, D])
    c_idf = din("c_idf", [128, 128])
    c_tri = din("c_tri", [128, 128])
    c_one = din("c_one", [128, 128])
    c_blk = din("c_blk", [128, 128])
    g2d = din("g2", [1, D])
    w_query = din("w_query", [D, 2048])
    skeys = din("skeys", [16, 128, 128])
    c_iota = din("c_iota", [1, 16])
    exuv = din("exuv", [16384, 2 * D])
    exb = nc.dram_tensor("exb", [16384, 2 * D], BF16, kind="Internal").ap()
    outd = nc.dram_tensor("out", [NOWN * 128, D], F32, kind="ExternalOutput").ap()
    vscr = nc.dram_tensor("vscr", [(NHALO + NOWN) * 128, 520], BF16, kind="Internal").ap()
    oscr = nc.dram_tensor("oscr", [3, NOWN * 128, 520], F32, kind="Internal").ap()

    es = ExitStack()
    with es:
        esA = ExitStack()
        with esA:
            P = Prog(nc, "A")
            banks = [esA.enter_context(nc.psum_tensor(f"bank{i}", [128, 512], F32)) for i in range(8)]
            bank_i = [0]

            def nbank():
                b = bank_i[0] % 8
                bank_i[0] += 1
                return banks[b], ('bank', b)

            def sb(name, shape, dt=F32):
                return esA.enter_context(nc.sbuf_tensor(name, list(shape), dt))

            def dma(q, out, in_, r, w, stream, final=False):
                P.add(q, lambda e: e.dma_start(out=out, in_=in_), r=r, w=w, stream=stream, final=final)

            def mm(out, lhsT, rhs, start, stop, r, w):
                P.add('pe', lambda e: e.matmul(out, lhsT, rhs, start=start, stop=stop), r=r, w=w)

            def tr(out, in_, ident, r, w):
                P.add('pe', lambda e: e.transpose(out, in_, ident), r=r, w=w)

            def act(out, in_, func, r, w, bias=None, scale=None, accum=None, eng='act'):
                kw = {}
                if bias is not None:
                    kw['bias'] = bias
                if scale is not None:
                    kw['scale'] = scale
                if accum is not None:
                    kw['accum_out'] = accum
                P.add(eng, lambda e: e.activation(out, in_, func, **kw), r=r, w=w)

            def tt(eng, out, in0, in1, op, r, w):
                P.add(eng, lambda e: e.tensor_tensor(out, in0, in1, op), r=r, w=w)

            def ts(eng, out, in0, s1, s2, op0, op1, r, w):
                if op1 is None:
                    P.add(eng, lambda e: e.tensor_scalar(out, in0, s1, None, op0), r=r, w=w)
                else:
                    P.add(eng, lambda e: e.tensor_scalar(out, in0, s1, s2, op0, op1), r=r, w=w)

            def stt(out, in0, scalar, in1, op0, op1, r, w, accum=None):
                P.add('dve', lambda e: e.scalar_tensor_tensor(out, in0, scalar, in1, op0, op1, accum_out=accum), r=r, w=w)

            def cp(eng, out, in_, r, w):
                if eng == 'act':
                    P.add('act', lambda e: e.copy(out, in_), r=r, w=w)
                else:
                    P.add(eng, lambda e: e.tensor_copy(out, in_), r=r, w=w)

            def recip(out, in_, r, w):
                P.add('dve', lambda e: e.reciprocal(out, in_), r=r, w=w)

            def memset(eng, ap, val, w):
                P.add(eng, lambda e: e.memset(ap, val), w=w)

            w_in_sb = sb("w_in_sb", [128, 8, INC], BF16)
            w_out_sb = sb("w_out_sb", [128, 8, D], BF16)
            g1 = sb("g1", [128, 8])
            cw = sb("cw_sb", [128, 16])
            cb = sb("cb_sb", [128, 4])
            wq16 = sb("wq16", [128, 4, 128], BF16)
            wk16 = sb("wk16", [128, 4, 128], BF16)
            gb = sb("gb_sb", [128, 8])
            mg = sb("mg_sb", [128, 4])
            sk = sb("sk_sb", [128, 4])
            qg = sb("qg_sb", [128, 4])
            kg = sb("kg_sb", [128, 4])
            tb = sb("tb_sb", [128, 8 * 256])
            idf = sb("idf", [128, 128])
            idb = sb("idb", [128, 128], BF16)
            tri = sb("tri", [128, 128])
            one = sb("one", [128, 128])
            blk16 = sb("blk16", [128, 128], BF16)
            blkf = sb("blkf", [128, 128])
            fl = sb("fl", [128, 2])
            fl16 = sb("fl16", [128, 8], BF16)
            kT_all = sb("kT_all", [128, 4, 2 * 2048], BF16)
            qT_mb = sb("qT_mb", [128, 4, 2048], BF16)
            ymT_mb = sb("ymT_mb", [128, 4, 2048], BF16)
            Cst = sb("Cst", [128, 4, 129])
            Cb = sb("Cb", [128, 4, 129], BF16)
            uT = sb("uT", [128, 4, 131])
            junk = sb("junk", [128, 128], BF16)

            SU = 'setup'
            for (dst, src) in ((g1, g1c), (cw, cwd), (cb, cbd), (mg, mgd), (sk, skd), (qg, qgd), (kg, kgd),
                               (idf, c_idf), (tri, c_tri), (one, c_one), (blkf, c_blk), (fl, flg)):
                dma('sp', dst[:], src, r=(), w=(dst.name,), stream=SU, final=True)
            dma('sp', gb[:], gbd.partition_broadcast(128), r=(), w=('gb_sb',), stream=SU, final=True)
            xt = [sb(f"xt{i}", [128, D]) for i in range(2)]
            dma('sp', xt[0][:, 0:512], wqm.rearrange("p a b -> p (a b)"), r=(), w=('xt0',), stream='Lxt0')
            dma('sp', xt[1][:, 0:512], wkm.rearrange("p a b -> p (a b)"), r=(), w=('xt1',), stream='Lxt1')
            cp('dve', wq16[:].rearrange("p a b -> p (a b)"), xt[0][:, 0:512], r=('xt0',), w=('wq16',))
            cp('dve', wk16[:].rearrange("p a b -> p (a b)"), xt[1][:, 0:512], r=('xt1',), w=('wk16',))
            cp('dve', idb[:], idf[:], r=('idf',), w=('idb',))
            cp('dve', blk16[:], blkf[:], r=('blkf',), w=('blk16',))
            cp('dve', fl16[:], fl[:, 0:1].to_broadcast([128, 8]), r=('fl',), w=('fl16',))
            memset('dve', Cst[:], 0.0, w=('Cst',))
            memset('dve', Cb[:], 0.0, w=('Cb',))
            memset('dve', uT[:], 0.0, w=('uT',))
            si = 0
            for kc in range(8):
                for c0 in range(0, INC, 1024):
                    n = min(1024, INC - c0)
                    s = xt[si % 2]
                    dma('sp', s[:, 0:n], w_in[kc * 128:(kc + 1) * 128, c0:c0 + n], r=(), w=(s.name,), stream='L' + s.name)
                    if si % 2:
                        act(w_in_sb[:, kc, c0:c0 + n], s[:, 0:n], AF.Copy, r=(s.name, 'g1'), w=('w_in_sb1',), scale=g1[:, kc:kc + 1])
                    else:
                        ts('dve', w_in_sb[:, kc, c0:c0 + n], s[:, 0:n], g1[:, kc:kc + 1], None, ALU.mult, None,
                           r=(s.name, 'g1'), w=('w_in_sb0',))
                    si += 1
            for kc in range(8):
                s = xt[si % 2]
                dma('sp', s[:, 0:D], w_out[kc * 128:(kc + 1) * 128, :], r=(), w=(s.name,), stream='L' + s.name)
                cp('act' if si % 2 else 'dve', w_out_sb[:, kc, :], s[:, 0:D], r=(s.name,), w=('w_out_sb%d' % (si % 2),))
                si += 1

            hb = [sb("hb0", [128, D], BF16)] * 2
            hT = [sb(f"hT{i}", [128, 8, 128], BF16) for i in range(2)]
            ssq = sb("ssq", [128, 4])
            cv = sb("cv", [128, 4, 128])
            c16 = sb("c16", [128, 4, 128], BF16)
            gts = sb("gts", [128, 8])
            spl = sb("spl", [128, 8])
            gcol = sb("gcol", [128, 16])
            kk = sb("kk", [128, 4, 128], BF16)
            kT16 = sb("kT16", [128, 4, 128], BF16)
            qT16 = sb("qT16", [128, 4, 128], BF16)
            vt = sb("vt", [128, 4, 129], BF16)
            sm = sb("sm", [128, 4, 128], BF16)
            pc = sb("pc", [128, 32])
            hn = sb("hn", [128, 4, 128])
            szb = sb("szb", [128, 4, 128])
            skc = sb("skc", [128, 4, 128])
            sqa = sb("sqa", [128, 512], BF16)
            rra = sb("rra", [128, 512])
            vx = [sb(f"vx{i}", [128, 8, 65], BF16) for i in range(2)]
            vxh = [sb(f"vxh{i}", [128, 8, 65], BF16) for i in range(2)]
            for i in range(2):
                memset('pool', vx[i][:], 1.0, w=(vx[i].name,))
                cp('dve', vxh[i][:, :, 64], fl16[:], r=('fl16',), w=(vxh[i].name,))

            SC = 128.0 ** -0.5

            fronted = set()

            def tileA_front(t):
                if t in fronted or t >= NT:
                    return
                fronted.add(t)
                b = t % 2
                X, HB, HT = xt[b], hb[b], hT[b]
                dma('sp', X[:], xall[t * 128:(t + 1) * 128, :], r=(), w=(X.name,), stream='L' + X.name)
                act(HB[:], X[:], AF.Square, r=(X.name,), w=(HB.name, 'ssq0'), accum=ssq[:, 0:1])
                act(ssq[:, 1:2], ssq[:, 0:1], AF.Sqrt, r=('ssq0',), w=('ssq1',), bias=EPS, scale=1.0 / D)
                recip(ssq[:, 2:3], ssq[:, 1:2], r=('ssq1',), w=('ssq2',))
                ts('dve', HB[:], X[:], ssq[:, 2:3], None, ALU.mult, None, r=(X.name, 'ssq2'), w=(HB.name,))
                pT, kpT = nbank()
                pTb = pT[:].bitcast(BF16)
                for kc in range(8):
                    tr(pTb[:, kc * 128:(kc + 1) * 128], HB[:, kc * 128:(kc + 1) * 128], idb[:],
                       r=(HB.name, 'idb'), w=(kpT,))
                cp('act', HT[:].rearrange("p a b -> p (a b)"), pTb[:, 0:1024], r=(kpT,), w=(HT.name,))

            def tileA(t):
                tileA_front(t)
                tileA_front(t + 1)
                role = 'own' if t >= NPRE else ('halo' if t >= T0H else 'pre')
                b = t % 2
                X, HB, HT = xt[b], hb[b], hT[b]

                def fmaj(off):
                    ps, kps = nbank()
                    for fc in range(4):
                        for kc in range(8):
                            mm(ps[:, fc * 128:(fc + 1) * 128], w_in_sb[:, kc, off + fc * 128: off + (fc + 1) * 128],
                               HT[:, kc, :], kc == 0, kc == 7, r=('w_in_sb0', 'w_in_sb1', HT.name), w=(kps,))
                    return ps, kps

                def tmaj(off, n):
                    ps, kps = nbank()
                    for kc in range(8):
                        mm(ps[:, 0:n], HT[:, kc, :], w_in_sb[:, kc, off:off + n], kc == 0, kc == 7,
                           r=('w_in_sb0', 'w_in_sb1', HT.name), w=(kps,))
                    return ps, kps

                psu, kpsu = fmaj(0)
                cp('dve', uT[:, :, 0:3], uT[:, :, 128:131], r=('uT',), w=('uT',))
                cp('act', uT[:, :, 3:131], psu[:].rearrange("p (a b) -> p a b", a=4), r=(kpsu,), w=('uT',))
                for h in range(4):
                    ts('dve', cv[:, h, :], uT[:, h, 0:128], cw[:, h * 4:h * 4 + 1], cb[:, h:h + 1], ALU.mult, ALU.add,
                       r=('uT', 'cw_sb', 'cb_sb'), w=('cv',))
                    for j in range(1, 4):
                        stt(cv[:, h, :], uT[:, h, j:j + 128], cw[:, h * 4 + j:h * 4 + j + 1], cv[:, h, :],
                            ALU.mult, ALU.add, r=('uT', 'cv', 'cw_sb'), w=('cv',))
                act(c16[:], cv[:], AF.Silu, r=('cv',), w=('c16',))
                if role == 'own':
                    act(skc[:], cv[:], AF.Silu, r=('cv',), w=('skc',))
                if role in ('halo', 'own'):
                    A = (t - T0H) * 128
                    pos = ((A // 2048) % 2) * 2048 + A % 2048

                    def qknorm(off, gt, gname, dst, q):
                        ps, kps = fmaj(off)
                        act(sqa[:], ps[:], AF.Square, r=(kps,), w=('sqa',))
                        pss, kpss = nbank()
                        mm(pss[:], blk16[:], sqa[:], True, True, r=('blk16', 'sqa'), w=(kpss,))
                        if q:
                            act(rra[:], pss[:], AF.Sqrt, r=(kpss,), w=('rra',), bias=64.0 * EPS, scale=1.0)
                        else:
                            act(rra[:], pss[:], AF.Sqrt, r=(kpss,), w=('rra',), bias=EPS, scale=1.0 / 64)
                        recip(rra[:], rra[:], r=('rra',), w=('rra',))
                        for g in range(4):
                            stt(dst(g), ps[:, g * 128:(g + 1) * 128], gt[:, g:g + 1], rra[:, g * 128:(g + 1) * 128],
                                ALU.mult, ALU.mult, r=(kps, gname, 'rra'), w=(dst.key,))

                    def dk(g):
                        return kT_all[:, g, pos:pos + 128]
                    dk.key = 'kT_all'
                    qknorm(2056, kg, 'kg_sb', dk, False)
                    if role == 'own':
                        tl = (t - NPRE) % 16

                        def dq(g):
                            return qT_mb[:, g, tl * 128:(tl + 1) * 128]
                        dq.key = 'qT_mb'
                        qknorm(1544, qg, 'qg_sb', dq, True)
                    psv, kpsv = tmaj(2568, 512)
                    VX = (vx if role == 'own' else vxh)[b]
                    cp('act', VX[:, :, 0:64], psv[:].rearrange("p (a b) -> p a b", a=8), r=(kpsv,), w=(VX.name,))
                    dma('sp', vscr[A:A + 128, :], VX[:].rearrange("p a b -> p (a b)"), r=(VX.name,), w=('vscr',),
                        stream='S' + VX.name)
                if role == 'own':
                    psz, kpsz = fmaj(1024)
                    act(szb[:].rearrange("p a b -> p (a b)"), psz[:], AF.Sigmoid, r=(kpsz,), w=('szb',))
                psvm, kpsvm = tmaj(512, 512)
                psif, kpsif = tmaj(1536, 8)
                tt('dve', gts[:], psif[:, 0:8], gb[:], ALU.add, r=(kpsif, 'gb_sb'), w=('gts',))
                act(spl[:, 0:4], gts[:, 4:8], AF.Exp, r=('gts',), w=('spl0',), scale=-1.0)
                act(spl[:, 4:8], spl[:, 0:4], AF.Ln, r=('spl0',), w=('spl1',), bias=1.0)
                psb, kpsb = nbank()
                mm(psb[:, 0:4], tri[:], spl[:, 4:8], True, True, r=('tri', 'spl1'), w=(kpsb,))
                mm(psb[:, 4:8], one[:], spl[:, 4:8], True, True, r=('one', 'spl1'), w=(kpsb,))
                tt('dve', gcol[:, 0:4], psb[:, 0:4], gts[:, 0:4], ALU.add, r=(kpsb, 'gts'), w=('ga',))
                act(gcol[:, 4:8], gcol[:, 0:4], AF.Exp, r=('ga',), w=('gea',))
                act(gcol[:, 8:16], psb[:, 0:8], AF.Exp, r=(kpsb,), w=('geb',), scale=-1.0)
                psk, kpsk = nbank()
                for h in range(4):
                    mm(psk[:, h * 128:(h + 1) * 128], c16[:, h, :], wk16[:, h, :], True, True, r=('c16', 'wk16'), w=(kpsk,))
                for h in range(4):
                    ts('dve', kk[:, h, :], psk[:, h * 128:(h + 1) * 128], gcol[:, 12 + h:13 + h], SC, ALU.mult, ALU.mult,
                       r=(kpsk, 'geb'), w=('kk',))
                for h in range(4):
                    act(vt[:, h, 0:128], psvm[:, h * 128:(h + 1) * 128], AF.Copy, r=(kpsvm, 'gea'), w=('vt',),
                        scale=gcol[:, 4 + h:5 + h])
                cp('dve', vt[:, :, 128], gcol[:, 4:8], r=('gea',), w=('vt',))
                if role == 'own':
                    tl = (t - NPRE) % 16
                    psq, kpsq = nbank()
                    pskT, kpskT = nbank()
                    for h in range(4):
                        mm(psq[:, h * 128:(h + 1) * 128], wq16[:, h, :], c16[:, h, :], True, True, r=('c16', 'wq16'), w=(kpsq,))
                    for h in range(4):
                        mm(pskT[:, h * 128:(h + 1) * 128], wk16[:, h, :], c16[:, h, :], True, True, r=('c16', 'wk16'), w=(kpskT,))
                    cp('act', qT16[:].rearrange("p a b -> p (a b)"), psq[:], r=(kpsq,), w=('qT16',))
                    act(kT16[:].rearrange("p a b -> p (a b)"), pskT[:], AF.Copy, r=(kpskT,), w=('kT16',), scale=SC)
                    psS, kpsS = nbank()
                    for h in range(4):
                        mm(psS[:, h * 128:(h + 1) * 128], kT16[:, h, :], qT16[:, h, :], True, True, r=('kT16', 'qT16'), w=(kpsS,))
                    tt('dve', sm[:], psS[:].rearrange("p (a b) -> p a b", a=4),
                       tri[:].unsqueeze(1).to_broadcast([128, 4, 128]), ALU.mult, r=(kpsS, 'tri'), w=('sm',))
                    psN = [nbank(), nbank()]
                    for h in range(4):
                        pn, kpn = psN[h // 2]
                        o = pn[:, (h % 2) * 129:(h % 2) * 129 + 129]
                        mm(o, sm[:, h, :], vt[:, h, :], True, False, r=('sm', 'vt'), w=(kpn,))
                        mm(o, qT16[:, h, :], Cb[:, h, :], False, True, r=('qT16', 'Cb'), w=(kpn,))
                    for hp in range(2):
                        pn, kpn = psN[hp]
                        tt('dve', pc[:, 2 * hp:2 * hp + 2], pn[:, 128:258:129], gcol[:, 8 + 2 * hp:10 + 2 * hp], ALU.mult,
                           r=(kpn, 'geb'), w=('pc_tden',))
                    act(pc[:, 4:8], pc[:, 0:4], AF.Abs, r=('pc_tden',), w=('pc_t2',))
                    ts('dve', pc[:, 4:8], pc[:, 4:8], 1.0, None, ALU.max, None, r=('pc_t2',), w=('pc_t2',))
                    recip(pc[:, 8:12], pc[:, 4:8], r=('pc_t2',), w=('pc_rt',))
                    tt('dve', pc[:, 12:16], pc[:, 8:12], gcol[:, 8:12], ALU.mult, r=('pc_rt', 'geb'), w=('pc_sc',))
                    for h in range(4):
                        pn, kpn = psN[h // 2]
                        act(junk[:, 0:128], pn[:, (h % 2) * 129:(h % 2) * 129 + 128], AF.Square, r=(kpn,),
                            w=('pc_ssn',) + (('junk',) if STRICT else ()), accum=pc[:, 16 + h:17 + h])
                    tt('dve', pc[:, 20:24], pc[:, 12:16], pc[:, 12:16], ALU.mult, r=('pc_sc',), w=('pc_v',))
                    tt('dve', pc[:, 20:24], pc[:, 20:24], pc[:, 16:20], ALU.mult, r=('pc_v', 'pc_ssn'), w=('pc_v',))
                    act(pc[:, 24:28], pc[:, 20:24], AF.Sqrt, r=('pc_v',), w=('pc_sq',), bias=EPS, scale=1.0 / 128)
                    recip(pc[:, 24:28], pc[:, 24:28], r=('pc_sq',), w=('pc_sq',))
                    tt('dve', pc[:, 28:32], pc[:, 24:28], pc[:, 12:16], ALU.mult, r=('pc_sq', 'pc_sc'), w=('pc_fac',))
                    for h in range(4):
                        pn, kpn = psN[h // 2]
                        act(hn[:, h, :], pn[:, (h % 2) * 129:(h % 2) * 129 + 128], AF.Copy, r=(kpn, 'pc_fac'), w=('hn',),
                            scale=pc[:, 28 + h:29 + h])
                    psH, kpsH = nbank()
                    for h in range(4):
                        tr(psH[:, h * 128:(h + 1) * 128], hn[:, h, :], idf[:], r=('hn', 'idf'), w=(kpsH,))
                    for h in range(4):
                        ts('dve', skc[:, h, :], skc[:, h, :], sk[:, h:h + 1], None, ALU.mult, None, r=('skc', 'sk_sb'), w=('skc',))
                    for h in range(4):
                        stt(skc[:, h, :], psH[:, h * 128:(h + 1) * 128], mg[:, h:h + 1], skc[:, h, :], ALU.mult, ALU.add,
                            r=(kpsH, 'mg_sb', 'skc'), w=('skc',))
                    tt('dve', ymT_mb[:, :, tl * 128:(tl + 1) * 128], skc[:], szb[:], ALU.mult, r=('skc', 'szb'), w=('ymT_mb',))
                psP = [nbank(), nbank()]
                for h in range(4):
                    pp, kpp = psP[h // 2]
                    mm(pp[:, (h % 2) * 129:(h % 2) * 129 + 129], kk[:, h, :], vt[:, h, :], True, True, r=('kk', 'vt'), w=(kpp,))
                for h in range(4):
                    pp, kpp = psP[h // 2]
                    stt(Cst[:, h, :], Cst[:, h, :], gcol[:, 12 + h:13 + h], pp[:, (h % 2) * 129:(h % 2) * 129 + 129],
                        ALU.mult, ALU.add, r=('Cst', 'geb', kpp), w=('Cst',))
                if t == NPRE - 1:
                    ts('dve', Cst[:], Cst[:], fl[:, 0:1], None, ALU.mult, None, r=('Cst', 'fl'), w=('Cst',))
                cp('act', Cb[:], Cst[:], r=('Cst',), w=('Cb',))

            vk = [sb(f"vk{i}", [128, 8, 65], BF16) for i in range(4)]
            sbb = [sb(f"sbb{i}", [128, 4, 128]) for i in range(2)]
            pTt = [sb(f"pTt{i}", [128, 4, 128], BF16) for i in range(2)]
            osb = [sb(f"osb{i}", [128, 8, 65]) for i in range(2)]
            att_ctr = [0]

            SBs = [sbb[0], sbb[1], hn, szb]
            PTs = [pTt[0], pTt[1], sm, c16]

            def attention(m):
                A0 = 2048 * (m + 1)
                its = []
                for pi, (win, d) in enumerate(PATS):
                    span = 128 * d
                    first = True
                    for blk in range(2048 // span):
                        for rr_ in range(d):
                            i = att_ctr[0]
                            att_ctr[0] += 1
                            qa0 = A0 + blk * span + rr_
                            for hg in range(2):
                                its.append(dict(pi=pi, d=d, span=span, qa0=qa0, ka=(qa0 - span, qa0),
                                                vks=(vk[(2 * i) % 4], vk[(2 * i + 1) % 4]), ql=qa0 - A0,
                                                O=osb[i % 2], hg=hg, newpat=first and hg == 0))
                            first = False

                def stage1(n, I):
                    d, span, ka, vks, ql, hg, pi = I['d'], I['span'], I['ka'], I['vks'], I['ql'], I['hg'], I['pi']
                    if I['newpat']:
                        dma('sp', tb[:], tbd[pi], r=(), w=('tb_sb',), stream='Ltb')
                    if hg == 0:
                        for j in range(2):
                            dma('act', vks[j][:].rearrange("p a b -> p (a b)"), vscr[ka[j]:ka[j] + span - d + 1:d, :],
                                r=('vscr',), w=(vks[j].name,), stream='L' + vks[j].name)
                    for j in range(2):
                        S_, PT_ = SBs[2 * (n % 2) + j], PTs[2 * (n % 2) + j]
                        kp = ((ka[j] // 2048) % 2) * 2048 + ka[j] % 2048
                        ps, kps = nbank()
                        for hh in range(4):
                            head = 2 * hh + hg
                            g, po = head // 2, (head % 2) * 64
                            mm(ps[:, hh * 128:(hh + 1) * 128], kT_all[po:po + 64, g, kp:kp + span - d + 1:d],
                               qT_mb[po:po + 64, g, ql:ql + span - d + 1:d], True, True, r=('kT_all', 'qT_mb'), w=(kps,))
                        c0 = 128 if j == 0 else 0
                        tbv = tb[:].rearrange("p (h c) -> p h c", h=8)[:, hg:8:2, c0:c0 + 128]
                        tt('dve', S_[:], ps[:].rearrange("p (a b) -> p a b", a=4), tbv, ALU.add,
                           r=(kps, 'tb_sb'), w=(S_.name,))
                        act(PT_[:], S_[:], AF.Exp, r=(S_.name,), w=(PT_.name,))

                def stage2(n, I):
                    d, span, vks, ql, hg, pi, O = I['d'], I['span'], I['vks'], I['ql'], I['hg'], I['pi'], I['O']
                    po_, kpo = nbank()
                    for hh in range(4):
                        head = 2 * hh + hg
                        for j in range(2):
                            PT_ = PTs[2 * (n % 2) + j]
                            mm(po_[:, hh * 65:hh * 65 + 65], PT_[:, hh, :], vks[j][:, head, :], j == 0, j == 1,
                               r=(PT_.name, vks[j].name), w=(kpo,))
                    cp('act' if hg else 'dve', O[:, hg:8:2, :],
                       po_[:, 0:260].rearrange("p (a b) -> p a b", a=4), r=(kpo,), w=(O.name,))
                    if hg == 1:
                        dma('sp', oscr[pi, ql + m * 2048:ql + m * 2048 + span - d + 1:d, :], O[:].rearrange("p a b -> p (a b)"),
                            r=(O.name,), w=('oscr',), stream='S' + O.name)

                stage1(0, its[0])
                for n in range(len(its)):
                    if n + 1 < len(its):
                        stage1(n + 1, its[n + 1])
                    stage2(n, its[n])

            o3 = [[sb(f"o3_{p}", [128, 8, 65]) for p in range(3)]] * 2
            ya16 = sb("ya16", [128, 8, 64], BF16)
            rden = sb("rden", [128, 8])
            yaT = sb("yaT", [128, 4, 128], BF16)

            def combine(m, tl):
                to = m * 16 + tl
                b = tl % 2
                os_ = o3[b]
                for p in range(3):
                    dma('act', os_[p][:].rearrange("p a b -> p (a b)"), oscr[p, to * 128:(to + 1) * 128, :],
                        r=('oscr',), w=(os_[p].name,), stream='L' + os_[p].name)
                X = xt[b]
                dma('sp', X[:], xall[(NPRE + to) * 128:(NPRE + to + 1) * 128, :], r=(), w=(X.name,), stream='L' + X.name)
                tt('dve', os_[0][:], os_[0][:], os_[1][:], ALU.add, r=(os_[0].name, os_[1].name), w=(os_[0].name,))
                tt('dve', os_[0][:], os_[0][:], os_[2][:], ALU.add, r=(os_[0].name, os_[2].name), w=(os_[0].name,))
                recip(rden[:], os_[0][:, :, 64], r=(os_[0].name,), w=('rden',))
                tt('dve', ya16[:], os_[0][:, :, 0:64], rden[:].unsqueeze(2).to_broadcast([128, 8, 64]), ALU.mult,
                   r=(os_[0].name, 'rden'), w=('ya16',))
                pT, kpT = nbank()
                pTb = pT[:].bitcast(BF16)
                yaf = ya16[:].rearrange("p a b -> p (a b)")
                for fc in range(4):
                    tr(pTb[:, fc * 128:(fc + 1) * 128], yaf[:, fc * 128:(fc + 1) * 128], idb[:], r=('ya16', 'idb'), w=(kpT,))
                cp('act', yaT[:].rearrange("p a b -> p (a b)"), pTb[:, 0:512], r=(kpT,), w=('yaT',))
                X1 = X
                for nh in range(2):
                    py, kpy = nbank()
                    for kc in range(8):
                        lhs = ymT_mb[:, kc, tl * 128:(tl + 1) * 128] if kc < 4 else yaT[:, kc - 4, :]
                        mm(py[:], lhs, w_out_sb[:, kc, nh * 512:(nh + 1) * 512], kc == 0, kc == 7,
                           r=('ymT_mb', 'yaT', 'w_out_sb0', 'w_out_sb1'), w=(kpy,))
                    tt('dve', X1[:, nh * 512:(nh + 1) * 512], py[:], X[:, nh * 512:(nh + 1) * 512], ALU.add,
                       r=(kpy, X.name), w=(X.name,))
                dma('sp', outd[to * 128:(to + 1) * 128, :], X1[:], r=(X1.name,), w=('outd',), stream='S' + X1.name)

            cst = sb("cst", [128, 512])
            cst16 = sb("cst16", [128, 512], BF16)
            conv_i = [0]
            NCH = 512

            def convert(n):
                for _ in range(n):
                    i = conv_i[0]
                    if i >= NCH:
                        return
                    conv_i[0] += 1
                    rb_, cb_ = i // 4, i % 4
                    dma('pool', cst[:], exuv[rb_ * 128:(rb_ + 1) * 128, cb_ * 512:(cb_ + 1) * 512], r=(), w=('cst',), stream='Lcst')
                    cp('pool', cst16[:], cst[:], r=('cst',), w=('cst16',))
                    dma('pool', exb[rb_ * 128:(rb_ + 1) * 128, cb_ * 512:(cb_ + 1) * 512], cst16[:], r=('cst16',), w=(), stream='Scst16')
            per_tile = -(-NCH // NT)
            _tileA = tileA

            def tileA(t):
                convert(per_tile)
                _tileA(t)
            import os
            KS = os.environ.get("KSTOP", "")
            if KS == "setup":
                pass
            elif KS == "t1":
                tileA(0)
            elif KS == "pre":
                for t in range(NPRE):
                    tileA(t)
            elif KS == "own1":
                for t in range(NPRE + 1):
                    tileA(t)
            elif KS == "att":
                for t in range(NPRE + 16):
                    tileA(t)
                attention(0)
            else:
                for t in range(NPRE):
                    tileA(t)
                for m in range(NOWN // 16):
                    for tl in range(16):
                        tileA(NPRE + m * 16 + tl)
                    attention(m)
                    for tl in range(16):
                        combine(m, tl)
                convert(NCH)
            nsA = P.emit(es)
        if with_peer:
            esB = ExitStack()
            with esB:
                buildB(nc, es, esB, NOWN, outd, g2d, w_query, skeys, c_iota, c_idf, exb)
    return nc


def buildB(nc, es, esB, NOWN, outd, g2d, w_query, skeys, c_iota, c_idf, exb):
    P = Prog(nc, "B")
    NG, NDG, GS = 24, 8, 4

    def sb(name, shape, dt=F32):
        return esB.enter_context(nc.sbuf_tensor(name, list(shape), dt))

    rbanks = [esB.enter_context(nc.psum_tensor(f"rb{i}", [128, 512], F32)) for i in range(4)]
    h2f = esB.enter_context(nc.psum_tensor("h2f", [128, D], F32))
    pout = [esB.enter_context(nc.psum_tensor(f"pout{i}", [128, 512], F32)) for i in range(2)]
    bank_i = [0]

    def nbank():
        b = bank_i[0] % 4
        bank_i[0] += 1
        return rbanks[b], ('rb', b)

    def dma(q, out, in_, r, w, stream, final=False):
        P.add(q, lambda e: e.dma_start(out=out, in_=in_), r=r, w=w, stream=stream, final=final)

    def mm(out, lhsT, rhs, start, stop, r, w):
        P.add('pe', lambda e: e.matmul(out, lhsT, rhs, start=start, stop=stop), r=r, w=w)

    def tr(out, in_, ident, r, w):
        P.add('pe', lambda e: e.transpose(out, in_, ident), r=r, w=w)

    def act(out, in_, func, r, w, bias=None, scale=None, accum=None):
        kw = {}
        if bias is not None:
            kw['bias'] = bias
        if scale is not None:
            kw['scale'] = scale
        if accum is not None:
            kw['accum_out'] = accum
        P.add('act', lambda e: e.activation(out, in_, func, **kw), r=r, w=w)

    def tt(eng, out, in0, in1, op, r, w):
        P.add(eng, lambda e: e.tensor_tensor(out, in0, in1, op), r=r, w=w)

    def ts(eng, out, in0, s1, s2, op0, op1, r, w):
        if op1 is None:
            P.add(eng, lambda e: e.tensor_scalar(out, in0, s1, None, op0), r=r, w=w)
        else:
            P.add(eng, lambda e: e.tensor_scalar(out, in0, s1, s2, op0, op1), r=r, w=w)

    def stt(out, in0, scalar, in1, op0, op1, r, w, accum=None):
        P.add('dve', lambda e: e.scalar_tensor_tensor(out, in0, scalar, in1, op0, op1, accum_out=accum), r=r, w=w)

    def cp(eng, out, in_, r, w):
        if eng == 'act':
            P.add('act', lambda e: e.copy(out, in_), r=r, w=w)
        else:
            P.add(eng, lambda e: e.tensor_copy(out, in_), r=r, w=w)

    def recip(out, in_, r, w):
        P.add('dve', lambda e: e.reciprocal(out, in_), r=r, w=w)

    def red(out, in_, op, r, w):
        P.add('dve', lambda e: e.tensor_reduce(out, in_, AX.X, op), r=r, w=w)

    wq_sb = sb("wq_sb", [128, 8, 2048], BF16)
    skT = sb("skT", [128, 16, 128], BF16)
    idf = sb("idfB", [128, 128])
    idb = sb("idbB", [128, 128], BF16)
    g2b = sb("g2b", [128, D])
    iot = sb("iot", [128, 16])
    x1t = [sb(f"x1t{i}", [128, D]) for i in range(3)]
    hb2 = sb("hb2", [128, D], BF16)
    h2T = sb("h2T", [128, 8, 128], BF16)
    junk = sb("junkB", [128, D], BF16)
    junkD = sb("junkD", [128, D], BF16)
    ssq = [sb(f"ssqB{i}", [128, 4]) for i in range(2)]
    qpT = sb("qpT", [128, 16, 128], BF16)
    sc = sb("sc", [128, 16, 128])
    sc2 = sb("sc2", [128, 16, 128])
    v16 = sb("v16", [128, 16, 16])
    i16 = sb("i16", [128, 16, 16], U32)
    i16f = sb("i16f", [128, 16, 16])
    cand = sb("cand", [128, 8, 256])
    cand2 = sc[:].rearrange("p (h a) k -> p h (a k)", h=8)
    vc = sb("vc", [128, 8, 16])
    ic = sb("ic", [128, 8, 16], U32)
    ich = sb("ich", [128, 8, 16], U32)
    icl = sb("icl", [128, 8, 16], U32)
    ichf = sb("ichf", [128, 8, 16])
    iclf = sb("iclf", [128, 8, 16])
    eq = sc2[:].rearrange("p (h a) (b c) -> p h (a b) c", h=8, c=16)
    e12 = sb("e12", [128, 2, 8, 16])
    idxf = sb("idxf", [128, 128])
    idx = [sb(f"idx{i}", [128, 128], I32) for i in range(2)]
    gate = [sb(f"gate{i}", [128, 128]) for i in range(2)]
    gtmp = sb("gtmp", [128, 8, 16])
    gs = sb("gs", [128, 16])
    pre = sb("pre", [128, 128])
    av = sb("av", [128, 128])
    dg = [sb(f"dg{i}", [128, 128], BF16) for i in range(NDG)]
    gbuf = [sb(f"gb{i}", [128, 2 * D], BF16) for i in range(NG)]
    osb = [sb(f"osbB{i}", [128, D]) for i in range(2)]

    SU = 'setupB'
    dma('sp', idf[:], c_idf, r=(), w=('idfB',), stream=SU, final=True)
    dma('sp', g2b[:], g2d.partition_broadcast(128), r=(), w=('g2b',), stream=SU, final=True)
    dma('sp', iot[:], c_iota.partition_broadcast(128), r=(), w=('iot',), stream=SU, final=True)
    cp('dve', idb[:], idf[:], r=('idfB',), w=('idbB',))
    si = 0
    for kc in range(8):
        for c0 in range(0, 2048, 1024):
            s_ = x1t[si % 2]
            dma('sp', s_[:], w_query[kc * 128:(kc + 1) * 128, c0:c0 + 1024], r=(), w=(s_.name,), stream='L' + s_.name)
            cp('act' if si % 2 else 'dve', wq_sb[:, kc, c0:c0 + 1024], s_[:], r=(s_.name,), w=('wq_sb%d' % (si % 2),))
            si += 1
    for q4 in range(4):
        s_ = x1t[si % 2]
        dma('sp', s_[:, 0:512].rearrange("p (a b) -> p a b", a=4), skeys[q4 * 4:(q4 + 1) * 4].rearrange("a k c -> k a c"),
            r=(), w=(s_.name,), stream='L' + s_.name)
        ps, kps = nbank()
        for a in range(4):
            tr(ps[:, a * 128:(a + 1) * 128], s_[:, a * 128:(a + 1) * 128], idf[:], r=(s_.name, 'idfB'), w=(kps,))
        cp('act', skT[:, q4 * 4:(q4 + 1) * 4, :].rearrange("p a b -> p (a b)"), ps[:], r=(kps,), w=('skT',))
        si += 1

    def FE(t):
        X = x1t[t % 3]
        SS = ssq[t % 2]
        IDX = idx[t % 2]
        G = gate[t % 2]
        dma('sp', X[:], outd[t * 128:(t + 1) * 128, :], r=(('outd', t),), w=(X.name,), stream='L' + X.name)
        act(junk[:], X[:], AF.Square, r=(X.name,), w=(SS.name + 'a',) + (('junkB',) if STRICT else ()), accum=SS[:, 0:1])
        act(SS[:, 1:2], SS[:, 0:1], AF.Sqrt, r=(SS.name + 'a',), w=(SS.name + 'b',), bias=EPS, scale=1.0 / D)
        yield
        recip(SS[:, 2:3], SS[:, 1:2], r=(SS.name + 'b',), w=(SS.name,))
        stt(hb2[:], X[:], SS[:, 2:3], g2b[:], ALU.mult, ALU.mult, r=(X.name, SS.name, 'g2b'), w=('hb2',))
        pT, kpT = nbank()
        pTb = pT[:].bitcast(BF16)
        for kc in range(8):
            tr(pTb[:, kc * 128:(kc + 1) * 128], hb2[:, kc * 128:(kc + 1) * 128], idb[:], r=('hb2', 'idbB'), w=(kpT,))
        cp('act', h2T[:].rearrange("p a b -> p (a b)"), pTb[:, 0:1024], r=(kpT,), w=('h2T',))
        yield
        for q4 in range(4):
            ps, kps = nbank()
            for a in range(4):
                blk = q4 * 4 + a
                for kc in range(8):
                    mm(ps[:, a * 128:(a + 1) * 128], wq_sb[:, kc, blk * 128:(blk + 1) * 128], h2T[:, kc, :], kc == 0, kc == 7,
                       r=('wq_sb0', 'wq_sb1', 'h2T'), w=(kps,))
            cp('act', qpT[:, q4 * 4:(q4 + 1) * 4, :].rearrange("p a b -> p (a b)"), ps[:], r=(kps,), w=('qpT',))
            yield
        for q4 in range(4):
            ps, kps = nbank()
            for a in range(4):
                blk = q4 * 4 + a
                mm(ps[:, a * 128:(a + 1) * 128], qpT[:, blk, :], skT[:, blk, :], True, True, r=('qpT', 'skT'), w=(kps,))
            cp('act', sc[:, q4 * 4:(q4 + 1) * 4, :].rearrange("p a b -> p (a b)"), ps[:], r=(kps,), w=('sc',))
            yield

        def top16_ops(src, src2, vout, iout, key, key2, tag):
            kv, ki, k2 = (key + 'v', tag), (key + 'i', tag), (key2, tag)
            return [
                (lambda e: e.max(vout[:, 0:8], src), (key,), (kv,)),
                (lambda e: e.max_index(iout[:, 0:8], vout[:, 0:8], src), (key, kv), (ki,)),
                (lambda e: e.match_replace(src2, vout[:, 0:8], src, -1e30), (key, kv), (k2,)),
                (lambda e: e.max(vout[:, 8:16], src2), (k2,), (kv,)),
                (lambda e: e.max_index(iout[:, 8:16], vout[:, 8:16], src2), (k2, kv), (ki,)),
            ]

        def emit_interleaved(lists):
            for grp_ops in zip(*lists):
                for fn, r_, w_ in grp_ops:
                    P.add('dve', fn, r=r_, w=w_)
                yield

        for b0 in range(0, 16, 8):
            yield from emit_interleaved([top16_ops(sc[:, blk, :], sc2[:, blk, :], v16[:, blk, :], i16[:, blk, :], 'sc', 'sc2', blk)
                                         for blk in range(b0, b0 + 8)])
        SCV = tuple(('scv', blk) for blk in range(16))
        SCI = tuple(('sci', blk) for blk in range(16))
        SC2 = tuple(('sc2', blk) for blk in range(16))
        EQK = ('sc2',) + SC2
        cp('dve', i16f[:], i16[:], r=SCI, w=('i16f',))
        v4 = v16[:].rearrange("p (h two) s -> p h two s", two=2)
        tt('dve', cand[:].rearrange("p h (i j) -> p h i j", i=16),
           v4[:, :, 0, :].unsqueeze(3).to_broadcast([128, 8, 16, 16]),
           v4[:, :, 1, :].unsqueeze(2).to_broadcast([128, 8, 16, 16]), ALU.add, r=SCV, w=('cand',))
        for h0 in range(0, 8, 8):
            lists = [top16_ops(cand[:, h, :], cand2[:, h, :], vc[:, h, :], ic[:, h, :], 'cand', 'scA', h) for h in range(h0, h0 + 8)]
            for ops in lists:
                fn, r_, w_ = ops[2]
                ops[2] = (fn, r_, w_ + ('sc',))
                for q in (3, 4):
                    fn, r_, w_ = ops[q]
                    ops[q] = (fn, r_ + ('sc',), w_)
            yield from emit_interleaved(lists)
        CV = tuple(('candv', h) for h in range(8))
        CI = tuple(('candi', h) for h in range(8))
        P.add('dve', lambda e: e.tensor_single_scalar(ich[:], ic[:], 4, ALU.logical_shift_right), r=CI, w=('ich',))
        P.add('dve', lambda e: e.tensor_single_scalar(icl[:], ic[:], 15, ALU.bitwise_and), r=CI, w=('icl',))
        cp('dve', ichf[:], ich[:], r=('ich',), w=('ichf',))
        cp('dve', iclf[:], icl[:], r=('icl',), w=('iclf',))
        i4 = i16f[:].rearrange("p (h two) s -> p h two s", two=2)
        iob = iot[:].unsqueeze(1).unsqueeze(1).to_broadcast([128, 8, 16, 16])
        for w_, srcf in ((0, ichf), (1, iclf)):
            tt('dve', eq, srcf[:].unsqueeze(3).to_broadcast([128, 8, 16, 16]), iob, ALU.is_equal,
               r=(srcf.name, 'iot'), w=EQK)
            tt('dve', eq, eq, i4[:, :, w_, :].unsqueeze(2).to_broadcast([128, 8, 16, 16]), ALU.mult,
               r=EQK + ('i16f',), w=EQK)
            red(e12[:, w_, :, :], eq, ALU.add, r=EQK, w=('e12',))
            yield
        stt(idxf[:], e12[:, 0, :, :].rearrange("p a b -> p (a b)"), 128.0, e12[:, 1, :, :].rearrange("p a b -> p (a b)"),
            ALU.mult, ALU.add, r=('e12',), w=('idxf',))
        cp('dve', IDX[:], idxf[:], r=('idxf',), w=(IDX.name,))
        tt('dve', gtmp[:], vc[:], vc[:, :, 0:1].to_broadcast([128, 8, 16]), ALU.subtract, r=CV, w=('gtmp',))
        act(gtmp[:], gtmp[:], AF.Exp, r=('gtmp',), w=('gtmp',))
        yield
        red(gs[:, 0:8], gtmp[:], ALU.add, r=('gtmp',), w=('gs',))
        recip(gs[:, 8:16], gs[:, 0:8], r=('gs',), w=('gs',))
        tt('dve', G[:].rearrange("p (a b) -> p a b", a=8), gtmp[:], gs[:, 8:16].unsqueeze(2).to_broadcast([128, 8, 16]),
           ALU.mult, r=('gtmp', 'gs'), w=(G.name,))

    gctr = [0]

    def BE(t, fe=None):
        X = x1t[t % 3]
        SS = ssq[t % 2]
        IDX = idx[t % 2]
        G = gate[t % 2]
        stt(h2f[:], X[:], SS[:, 2:3], g2b[:], ALU.mult, ALU.mult, r=(X.name, SS.name, 'g2b'), w=('h2f',))
        ngrp = 128 // GS

        def front(grp):
            bufs = []
            for s_ in range(GS):
                k = grp * GS + s_
                c = gctr[0]
                gctr[0] += 1
                B_ = gbuf[c % NG]
                bufs.append(B_)
                P.add('pool', lambda e, B_=B_, k=k: e.indirect_dma_start(
                    out=B_[:, :], out_offset=None, in_=exb[:, :],
                    in_offset=bass.IndirectOffsetOnAxis(ap=IDX[:, k:k + 1], axis=0)),
                    r=(IDX.name,), w=(B_.name,), stream='G' + B_.name)
                if STRICT:
                    stt(B_[:, 0:D], B_[:, 0:D], 1.0, h2f[:], ALU.mult, ALU.mult, r=(B_.name, 'h2f'),
                        w=(('pre', grp, s_), B_.name), accum=pre[:, k:k + 1])
                else:
                    stt(junkD[:], B_[:, 0:D], 1.0, h2f[:], ALU.mult, ALU.mult, r=(B_.name, 'h2f'), w=(('pre', grp, s_),),
                        accum=pre[:, k:k + 1])
            return bufs

        def back(grp, bufs):
            g0 = grp * GS
            act(av[:, g0:g0 + GS], pre[:, g0:g0 + GS], AF.Gelu, r=tuple(('pre', grp, q) for q in range(GS)), w=(('av', grp),))
            tt('dve', av[:, g0:g0 + GS], av[:, g0:g0 + GS], G[:, g0:g0 + GS], ALU.mult, r=(('av', grp), G.name), w=(('av', grp),))
            for s_ in range(GS):
                k = g0 + s_
                B_ = bufs[s_]
                DG = dg[k % NDG]
                act(DG[:], idf[:], AF.Copy, r=('idfB', ('av', grp)), w=(DG.name,), scale=av[:, k:k + 1])
                for nh in range(2):
                    mm(pout[nh][:], DG[:], B_[:, D + nh * 512:D + (nh + 1) * 512], k == 0, k == 127,
                       r=(DG.name, B_.name), w=('pout%d' % nh,))

        pend = []
        for grp in range(ngrp):
            pend.append((grp, front(grp)))
            if fe is not None:
                for _ in range(2):
                    next(fe, None)
            if len(pend) > 2:
                back(*pend.pop(0))
        for p_ in pend:
            back(*p_)
        if fe is not None:
            for _ in fe:
                pass
        O = osb[t % 2]
        for nh in range(2):
            tt('dve', O[:, nh * 512:(nh + 1) * 512], pout[nh][:], X[:, nh * 512:(nh + 1) * 512], ALU.add,
               r=('pout%d' % nh, X.name), w=(O.name,))
        dma('sp', outd[t * 128:(t + 1) * 128, :], O[:], r=(O.name,), w=(('outd', t),), stream='S' + O.name)

    for _ in FE(0):
        pass
    for t in range(NOWN):
        BE(t, FE(t + 1) if t + 1 < NOWN else None)
    P.emit(es)


def _bucket_table(rel_bias):
    nb, maxd = 32, 2048
    me = nb // 2
    out = np.full((3, 128, 8, 256), NEG, np.float32)
    j = np.arange(128)[:, None]
    c = np.arange(256)[None, :]
    steps = c - j
    valid = (steps >= 0) & (steps <= 128)
    for pi, (win, d) in enumerate(PATS):
        dist = np.maximum(steps, 0) * d
        nf = np.maximum(dist, 1).astype(np.float32)
        large = me + (np.log(nf / me) / np.log(maxd / me) * (nb - me)).astype(np.int32)
        large = np.minimum(large, nb - 1)
        bidx = np.where(dist < me, dist, large)
        for h in range(8):
            out[pi, :, h, :] = np.where(valid, rel_bias[bidx, h], NEG)
    return out.reshape(3, 128, 8 * 256)


def make_in_maps(inp, cores, NPRE, NOWN):
    f = lambda a: np.ascontiguousarray(a, dtype=np.float32)
    ii = np.arange(128)
    common = {
        "w_in": f(inp['w_in'][0]),
        "g1c": f(inp['norm1_g'][0].reshape(8, 128).T),
        "cw": f(inp['conv_w'][0].reshape(4, 4, 128).transpose(2, 1, 0).reshape(128, 16)),
        "cb": f(inp['conv_b'][0].reshape(4, 128).T),
        "wqm": f(inp['wq_m'][0].transpose(1, 0, 2)),
        "wkm": f(inp['wk_m'][0].transpose(1, 0, 2)),
        "gb": f(np.concatenate([inp['ig_b'][0], inp['fg_b'][0]])[None, :]),
        "mg": f(inp['mh_norm_g'][0].T),
        "sk": f(inp['skip_m'][0].T),
        "qg": f(inp['qn_g'][0].reshape(4, 128).T),
        "kg": f(inp['kn_g'][0].reshape(4, 128).T),
        "tb": _bucket_table(np.asarray(inp['rel_bias'], np.float32)),
        "w_out": f(inp['w_out'][0]),
        "c_idf": f(np.eye(128)),
        "c_tri": f(ii[:, None] <= ii[None, :]),
        "c_one": f(np.ones((128, 128))),
        "c_blk": f((ii[:, None] // 64) == (ii[None, :] // 64)),
        "g2": f(inp['norm2_g'][0][None, :]),
        "w_query": f(inp['w_query'][0]),
        "skeys": f(np.stack([inp['sub_keys1'][0], inp['sub_keys2'][0]], axis=1).reshape(16, 128, 128)),
        "c_iota": f(np.arange(16)[None, :]),
        "exuv": f(np.concatenate([inp['expert_u'][0], inp['expert_v'][0]], axis=1)),
    }
    maps = []
    for xp, xo, flag in cores:
        m = dict(common)
        m["xall"] = f(np.concatenate([xp, xo], axis=0))
        m["flg"] = np.full((128, 2), float(flag), np.float32)
        maps.append(m)
    return maps


def kernel(**inp):
    inp = {k: np.asarray(v) for k, v in inp.items()}
    x = inp['x']
    B, S, _ = x.shape
    NPRE, NOWN = 32, 32
    half = NOWN * 128
    cores = []
    for b in range(B):
        cores.append((np.zeros((NPRE * 128, D), np.float32), x[b, 0:half], 0.0))
        cores.append((x[b, 0:half], x[b, half:2 * half], 1.0))
    nc = build(NPRE, NOWN, with_peer=True)
    in_maps = make_in_maps(inp, cores, NPRE, NOWN)
    res = run_bass_kernel_spmd(nc, in_maps, core_ids=list(range(len(cores))))
    out = np.zeros((B, S, D), np.float32)
    for c in range(len(cores)):
        b, hf = c // 2, c % 2
        out[b, hf * half:(hf + 1) * half] = res.results[c]["out"]
    return out
```

```python
import numpy as np
from contextlib import ExitStack
import concourse.bass as bass
import concourse.mybir as mybir
from concourse.bass_utils import run_bass_kernel_spmd

F32 = mybir.dt.float32
BF16 = mybir.dt.bfloat16
I32 = mybir.dt.int32
U32 = mybir.dt.uint32
AF = mybir.ActivationFunctionType
ALU = mybir.AluOpType
AX = mybir.AxisListType

D = 1024
INC = 3080
EPS = 1e-6
NEG = -30000.0
PATS = ((128, 1), (512, 4), (2048, 16))
W_ROT = 30000
STRICT = False
ENGS = ['pe', 'act', 'dve', 'pool', 'sp']


class Prog:
    def __init__(self, nc, name):
        self.nc = nc
        self.name = name
        self.ops = {e: [] for e in ENGS}
        self.cnt = {e: 0 for e in ENGS}
        self.lastw = {}
        self.readers = {}
        self.waited = {e: {} for e in ENGS}
        self.scnt = {}
        self.final_streams = set()

    def add(self, e, fn, r=(), w=(), stream=None, final=False):
        deps = {}
        def dep(tok):
            if tok is None:
                return
            sid, v = tok
            if v is None:
                deps[sid] = None
            elif sid not in deps or (deps[sid] is not None and deps[sid] < v):
                deps[sid] = v
        def same_eng(tok):
            if STRICT:
                return False
            return tok is not None and tok[0][0] == 'E' and tok[0][1] == e
        for k in r:
            dep(self.lastw.get(k))
        for k in w:
            t0 = self.lastw.get(k)
            if not same_eng(t0):
                dep(t0)
            for t in self.readers.get(k, ()):
                if not same_eng(t):
                    dep(t)
        if stream is None:
            idx = self.cnt[e]
            self.cnt[e] += 1
            j = idx // W_ROT
            tok = (('E', e, j), idx - j * W_ROT + 1)
            inc = (tok[0], 1)
        else:
            n = self.scnt.get(stream, 0) + 1
            self.scnt[stream] = n
            if final:
                self.final_streams.add(stream)
                tok = (('D', stream), None)
            else:
                tok = (('D', stream), 16 * n)
            inc = (('D', stream), 16)
        waits = []
        for sid, v in deps.items():
            if e == 'pe' and sid[0] == 'E' and sid[1] == 'pe':
                continue
            if v is None:
                if self.waited[e].get(sid) == 'F':
                    continue
                self.waited[e][sid] = 'F'
                waits.append((sid, None))
                continue
            pv = self.waited[e].get(sid, 0)
            if pv == 'F' or pv >= v:
                continue
            self.waited[e][sid] = v
            waits.append((sid, v))
        self.ops[e].append((waits, fn, inc))
        for k in w:
            self.lastw[k] = tok
            self.readers[k] = []
        for k in r:
            if k not in w:
                self.readers.setdefault(k, []).append(tok)
        return tok

    def emit(self, es, final_wait_engine='sp'):
        nc = self.nc
        sems = {}

        def get(sid):
            if sid not in sems:
                nm = self.name + '_' + '_'.join(str(s) for s in sid)
                sems[sid] = es.enter_context(nc.semaphore(nm.replace(':', '_').replace('-', 'm')))
            return sems[sid]

        fin = []
        for st, n in self.scnt.items():
            fin.append((('D', st), 16 * n))
        self.ops[final_wait_engine].append((fin, None, None))
        with nc.Block() as block:
            decos = {'pe': block.tensor, 'act': block.scalar, 'dve': block.vector,
                     'pool': block.gpsimd, 'sp': block.sync}
            for e in ENGS:
                ops = self.ops[e]

                def body(eng, ops=ops):
                    for waits, fn, inc in ops:
                        for sid, v in waits:
                            if v is None:
                                v = 16 * self.scnt[sid[1]]
                            eng.wait_ge(get(sid), v)
                        if fn is None:
                            continue
                        ins = fn(eng)
                        ins.then_inc(get(inc[0]), inc[1])
                decos[e](body)
        return len(sems)


def build(NPRE, NOWN, with_peer=True):
    NT = NPRE + NOWN
    NHALO = 16
    assert NPRE >= NHALO and NPRE % 16 == 0 and NOWN % 16 == 0
    T0H = NPRE - NHALO
    nc = bass.Bass("TRN2", target_bir_lowering=False)

    def din(name, shape, dt=F32):
        return nc.dram_tensor(name, list(shape), dt, kind="ExternalInput").ap()

    xall = din("xall", [NT * 128, D])
    flg = din("flg", [128, 2])
    w_in = din("w_in", [D, INC])
    g1c = din("g1c", [128, 8])
    cwd = din("cw", [128, 16])
    cbd = din("cb", [128, 4])
    wqm = din("wqm", [128, 4, 128])
    wkm = din("wkm", [128, 4, 128])
    gbd = din("gb", [1, 8])
    mgd = din("mg", [128, 4])
    skd = din("sk", [128, 4])
    qgd = din("qg", [128, 4])
    kgd = din("kg", [128, 4])
    tbd = din("tb", [3, 128, 8 * 256])
    w_out = din("w_out", [D, D])
    c_idf = din("c_idf", [128, 128])
    c_tri = din("c_tri", [128, 128])
    c_one = din("c_one", [128, 128])
    c_blk = din("c_blk", [128, 128])
    g2d = din("g2", [1, D])
    w_query = din("w_query", [D, 2048])
    skeys = din("skeys", [16, 128, 128])
    c_iota = din("c_iota", [1, 16])
    exuv = din("exuv", [16384, 2 * D])
    exb = nc.dram_tensor("exb", [16384, 2 * D], BF16, kind="Internal").ap()
    outd = nc.dram_tensor("out", [NOWN * 128, D], F32, kind="ExternalOutput").ap()
    vscr = nc.dram_tensor("vscr", [(NHALO + NOWN) * 128, 520], BF16, kind="Internal").ap()
    oscr = nc.dram_tensor("oscr", [3, NOWN * 128, 520], F32, kind="Internal").ap()

    es = ExitStack()
    with es:
        esA = ExitStack()
        with esA:
            P = Prog(nc, "A")
            banks = [esA.enter_context(nc.psum_tensor(f"bank{i}", [128, 512], F32)) for i in range(8)]
            bank_i = [0]

            def nbank():
                b = bank_i[0] % 8
                bank_i[0] += 1
                return banks[b], ('bank', b)

            def sb(name, shape, dt=F32):
                return esA.enter_context(nc.sbuf_tensor(name, list(shape), dt))

            def dma(q, out, in_, r, w, stream, final=False):
                P.add(q, lambda e: e.dma_start(out=out, in_=in_), r=r, w=w, stream=stream, final=final)

            def mm(out, lhsT, rhs, start, stop, r, w):
                P.add('pe', lambda e: e.matmul(out, lhsT, rhs, start=start, stop=stop), r=r, w=w)

            def tr(out, in_, ident, r, w):
                P.add('pe', lambda e: e.transpose(out, in_, ident), r=r, w=w)

            def act(out, in_, func, r, w, bias=None, scale=None, accum=None, eng='act'):
                kw = {}
                if bias is not None:
                    kw['bias'] = bias
                if scale is not None:
                    kw['scale'] = scale
                if accum is not None:
                    kw['accum_out'] = accum
                P.add(eng, lambda e: e.activation(out, in_, func, **kw), r=r, w=w)

            def tt(eng, out, in0, in1, op, r, w):
                P.add(eng, lambda e: e.tensor_tensor(out, in0, in1, op), r=r, w=w)

            def ts(eng, out, in0, s1, s2, op0, op1, r, w):
                if op1 is None:
                    P.add(eng, lambda e: e.tensor_scalar(out, in0, s1, None, op0), r=r, w=w)
                else:
                    P.add(eng, lambda e: e.tensor_scalar(out, in0, s1, s2, op0, op1), r=r, w=w)

            def stt(out, in0, scalar, in1, op0, op1, r, w, accum=None):
                P.add('dve', lambda e: e.scalar_tensor_tensor(out, in0, scalar, in1, op0, op1, accum_out=accum), r=r, w=w)

            def cp(eng, out, in_, r, w):
                if eng == 'act':
                    P.add('act', lambda e: e.copy(out, in_), r=r, w=w)
                else:
                    P.add(eng, lambda e: e.tensor_copy(out, in_), r=r, w=w)

            def recip(out, in_, r, w):
                P.add('dve', lambda e: e.reciprocal(out, in_), r=r, w=w)

            def memset(eng, ap, val, w):
                P.add(eng, lambda e: e.memset(ap, val), w=w)

            w_in_sb = sb("w_in_sb", [128, 8, INC], BF16)
            w_out_sb = sb("w_out_sb", [128, 8, D], BF16)
            g1 = sb("g1", [128, 8])
            cw = sb("cw_sb", [128, 16])
            cb = sb("cb_sb", [128, 4])
            wq16 = sb("wq16", [128, 4, 128], BF16)
            wk16 = sb("wk16", [128, 4, 128], BF16)
            gb = sb("gb_sb", [128, 8])
            mg = sb("mg_sb", [128, 4])
            sk = sb("sk_sb", [128, 4])
            qg = sb("qg_sb", [128, 4])
            kg = sb("kg_sb", [128, 4])
            tb = sb("tb_sb", [128, 8 * 256])
            idf = sb("idf", [128, 128])
            idb = sb("idb", [128, 128], BF16)
            tri = sb("tri", [128, 128])
            one = sb("one", [128, 128])
            blk16 = sb("blk16", [128, 128], BF16)
            blkf = sb("blkf", [128, 128])
            fl = sb("fl", [128, 2])
            fl16 = sb("fl16", [128, 8], BF16)
            kT_all = sb("kT_all", [128, 4, 2 * 2048], BF16)
            qT_mb = sb("qT_mb", [128, 4, 2048], BF16)
            ymT_mb = sb("ymT_mb", [128, 4, 2048], BF16)
            Cst = sb("Cst", [128, 4, 129])
            Cb = sb("Cb", [128, 4, 129], BF16)
            uT = sb("uT", [128, 4, 131])
            junk = sb("junk", [128, 128], BF16)

            SU = 'setup'
            for (dst, src) in ((g1, g1c), (cw, cwd), (cb, cbd), (mg, mgd), (sk, skd), (qg, qgd), (kg, kgd),
                               (idf, c_idf), (tri, c_tri), (one, c_one), (blkf, c_blk), (fl, flg)):
                dma('sp', dst[:], src, r=(), w=(dst.name,), stream=SU, final=True)
            dma('sp', gb[:], gbd.partition_broadcast(128), r=(), w=('gb_sb',), stream=SU, final=True)
            xt = [sb(f"xt{i}", [128, D]) for i in range(2)]
            dma('sp', xt[0][:, 0:512], wqm.rearrange("p a b -> p (a b)"), r=(), w=('xt0',), stream='Lxt0')
            dma('sp', xt[1][:, 0:512], wkm.rearrange("p a b -> p (a b)"), r=(), w=('xt1',), stream='Lxt1')
            cp('dve', wq16[:].rearrange("p a b -> p (a b)"), xt[0][:, 0:512], r=('xt0',), w=('wq16',))
            cp('dve', wk16[:].rearrange("p a b -> p (a b)"), xt[1][:, 0:512], r=('xt1',), w=('wk16',))
            cp('dve', idb[:], idf[:], r=('idf',), w=('idb',))
            cp('dve', blk16[:], blkf[:], r=('blkf',), w=('blk16',))
            cp('dve', fl16[:], fl[:, 0:1].to_broadcast([128, 8]), r=('fl',), w=('fl16',))
            memset('dve', Cst[:], 0.0, w=('Cst',))
            memset('dve', Cb[:], 0.0, w=('Cb',))
            memset('dve', uT[:], 0.0, w=('uT',))
            si = 0
            for kc in range(8):
                for c0 in range(0, INC, 1024):
                    n = min(1024, INC - c0)
                    s = xt[si % 2]
                    dma('sp', s[:, 0:n], w_in[kc * 128:(kc + 1) * 128, c0:c0 + n], r=(), w=(s.name,), stream='L' + s.name)
                    if si % 2:
                        act(w_in_sb[:, kc, c0:c0 + n], s[:, 0:n], AF.Copy, r=(s.name, 'g1'), w=('w_in_sb1',), scale=g1[:, kc:kc + 1])
                    else:
                        ts('dve', w_in_sb[:, kc, c0:c0 + n], s[:, 0:n], g1[:, kc:kc + 1], None, ALU.mult, None,
                           r=(s.name, 'g1'), w=('w_in_sb0',))
                    si += 1
            for kc in range(8):
                s = xt[si % 2]
                dma('sp', s[:, 0:D], w_out[kc * 128:(kc + 1) * 128, :], r=(), w=(s.name,), stream='L' + s.name)
                cp('act' if si % 2 else 'dve', w_out_sb[:, kc, :], s[:, 0:D], r=(s.name,), w=('w_out_sb%d' % (si % 2),))
                si += 1

            hb = [sb("hb0", [128, D], BF16)] * 2
            hT = [sb(f"hT{i}", [128, 8, 128], BF16) for i in range(2)]
            ssq = sb("ssq", [128, 4])
            cv = sb("cv", [128, 4, 128])
            c16 = sb("c16", [128, 4, 128], BF16)
            gts = sb("gts", [128, 8])
            spl = sb("spl", [128, 8])
            gcol = sb("gcol", [128, 16])
            kk = sb("kk", [128, 4, 128], BF16)
            kT16 = sb("kT16", [128, 4, 128], BF16)
            qT16 = sb("qT16", [128, 4, 128], BF16)
            vt = sb("vt", [128, 4, 129], BF16)
            sm = sb("sm", [128, 4, 128], BF16)
            pc = sb("pc", [128, 32])
            hn = sb("hn", [128, 4, 128])
            szb = sb("szb", [128, 4, 128])
            skc = sb("skc", [128, 4, 128])
            sqa = sb("sqa", [128, 512], BF16)
            rra = sb("rra", [128, 512])
            vx = [sb(f"vx{i}", [128, 8, 65], BF16) for i in range(2)]
            vxh = [sb(f"vxh{i}", [128, 8, 65], BF16) for i in range(2)]
            for i in range(2):
                memset('pool', vx[i][:], 1.0, w=(vx[i].name,))
                cp('dve', vxh[i][:, :, 64], fl16[:], r=('fl16',), w=(vxh[i].name,))

            SC = 128.0 ** -0.5

            fronted = set()

            def tileA_front(t):
                if t in fronted or t >= NT:
                    return
                fronted.add(t)
                b = t % 2
                X, HB, HT = xt[b], hb[b], hT[b]
                dma('sp', X[:], xall[t * 128:(t + 1) * 128, :], r=(), w=(X.name,), stream='L' + X.name)
                act(HB[:], X[:], AF.Square, r=(X.name,), w=(HB.name, 'ssq0'), accum=ssq[:, 0:1])
                act(ssq[:, 1:2], ssq[:, 0:1], AF.Sqrt, r=('ssq0',), w=('ssq1',), bias=EPS, scale=1.0 / D)
                recip(ssq[:, 2:3], ssq[:, 1:2], r=('ssq1',), w=('ssq2',))
                ts('dve', HB[:], X[:], ssq[:, 2:3], None, ALU.mult, None, r=(X.name, 'ssq2'), w=(HB.name,))
                pT, kpT = nbank()
                pTb = pT[:].bitcast(BF16)
                for kc in range(8):
                    tr(pTb[:, kc * 128:(kc + 1) * 128], HB[:, kc * 128:(kc + 1) * 128], idb[:],
                       r=(HB.name, 'idb'), w=(kpT,))
                cp('act', HT[:].rearrange("p a b -> p (a b)"), pTb[:, 0:1024], r=(kpT,), w=(HT.name,))

            def tileA(t):
                tileA_front(t)
                tileA_front(t + 1)
                role = 'own' if t >= NPRE else ('halo' if t >= T0H else 'pre')
                b = t % 2
                X, HB, HT = xt[b], hb[b], hT[b]

                def fmaj(off):
                    ps, kps = nbank()
                    for fc in range(4):
                        for kc in range(8):
                            mm(ps[:, fc * 128:(fc + 1) * 128], w_in_sb[:, kc, off + fc * 128: off + (fc + 1) * 128],
                               HT[:, kc, :], kc == 0, kc == 7, r=('w_in_sb0', 'w_in_sb1', HT.name), w=(kps,))
                    return ps, kps

                def tmaj(off, n):
                    ps, kps = nbank()
                    for kc in range(8):
                        mm(ps[:, 0:n], HT[:, kc, :], w_in_sb[:, kc, off:off + n], kc == 0, kc == 7,
                           r=('w_in_sb0', 'w_in_sb1', HT.name), w=(kps,))
                    return ps, kps

                psu, kpsu = fmaj(0)
                cp('dve', uT[:, :, 0:3], uT[:, :, 128:131], r=('uT',), w=('uT',))
                cp('act', uT[:, :, 3:131], psu[:].rearrange("p (a b) -> p a b", a=4), r=(kpsu,), w=('uT',))
                for h in range(4):
                    ts('dve', cv[:, h, :], uT[:, h, 0:128], cw[:, h * 4:h * 4 + 1], cb[:, h:h + 1], ALU.mult, ALU.add,
                       r=('uT', 'cw_sb', 'cb_sb'), w=('cv',))
                    for j in range(1, 4):
                        stt(cv[:, h, :], uT[:, h, j:j + 128], cw[:, h * 4 + j:h * 4 + j + 1], cv[:, h, :],
                            ALU.mult, ALU.add, r=('uT', 'cv', 'cw_sb'), w=('cv',))
                act(c16[:], cv[:], AF.Silu, r=('cv',), w=('c16',))
                if role == 'own':
                    act(skc[:], cv[:], AF.Silu, r=('cv',), w=('skc',))
                if role in ('halo', 'own'):
                    A = (t - T0H) * 128
                    pos = ((A // 2048) % 2) * 2048 + A % 2048

                    def qknorm(off, gt, gname, dst, q):
                        ps, kps = fmaj(off)
                        act(sqa[:], ps[:], AF.Square, r=(kps,), w=('sqa',))
                        pss, kpss = nbank()
                        mm(pss[:], blk16[:], sqa[:], True, True, r=('blk16', 'sqa'), w=(kpss,))
                        if q:
                            act(rra[:], pss[:], AF.Sqrt, r=(kpss,), w=('rra',), bias=64.0 * EPS, scale=1.0)
                        else:
                            act(rra[:], pss[:], AF.Sqrt, r=(kpss,), w=('rra',), bias=EPS, scale=1.0 / 64)
                        recip(rra[:], rra[:], r=('rra',), w=('rra',))
                        for g in range(4):
                            stt(dst(g), ps[:, g * 128:(g + 1) * 128], gt[:, g:g + 1], rra[:, g * 128:(g + 1) * 128],
                                ALU.mult, ALU.mult, r=(kps, gname, 'rra'), w=(dst.key,))

                    def dk(g):
                        return kT_all[:, g, pos:pos + 128]
                    dk.key = 'kT_all'
                    qknorm(2056, kg, 'kg_sb', dk, False)
                    if role == 'own':
                        tl = (t - NPRE) % 16

                        def dq(g):
                            return qT_mb[:, g, tl * 128:(tl + 1) * 128]
                        dq.key = 'qT_mb'
                        qknorm(1544, qg, 'qg_sb', dq, True)
                    psv, kpsv = tmaj(2568, 512)
                    VX = (vx if role == 'own' else vxh)[b]
                    cp('act', VX[:, :, 0:64], psv[:].rearrange("p (a b) -> p a b", a=8), r=(kpsv,), w=(VX.name,))
                    dma('sp', vscr[A:A + 128, :], VX[:].rearrange("p a b -> p (a b)"), r=(VX.name,), w=('vscr',),
                        stream='S' + VX.name)
                if role == 'own':
                    psz, kpsz = fmaj(1024)
                    act(szb[:].rearrange("p a b -> p (a b)"), psz[:], AF.Sigmoid, r=(kpsz,), w=('szb',))
                psvm, kpsvm = tmaj(512, 512)
                psif, kpsif = tmaj(1536, 8)
                tt('dve', gts[:], psif[:, 0:8], gb[:], ALU.add, r=(kpsif, 'gb_sb'), w=('gts',))
                act(spl[:, 0:4], gts[:, 4:8], AF.Exp, r=('gts',), w=('spl0',), scale=-1.0)
                act(spl[:, 4:8], spl[:, 0:4], AF.Ln, r=('spl0',), w=('spl1',), bias=1.0)
                psb, kpsb = nbank()
                mm(psb[:, 0:4], tri[:], spl[:, 4:8], True, True, r=('tri', 'spl1'), w=(kpsb,))
                mm(psb[:, 4:8], one[:], spl[:, 4:8], True, True, r=('one', 'spl1'), w=(kpsb,))
                tt('dve', gcol[:, 0:4], psb[:, 0:4], gts[:, 0:4], ALU.add, r=(kpsb, 'gts'), w=('ga',))
                act(gcol[:, 4:8], gcol[:, 0:4], AF.Exp, r=('ga',), w=('gea',))
                act(gcol[:, 8:16], psb[:, 0:8], AF.Exp, r=(kpsb,), w=('geb',), scale=-1.0)
                psk, kpsk = nbank()
                for h in range(4):
                    mm(psk[:, h * 128:(h + 1) * 128], c16[:, h, :], wk16[:, h, :], True, True, r=('c16', 'wk16'), w=(kpsk,))
                for h in range(4):
                    ts('dve', kk[:, h, :], psk[:, h * 128:(h + 1) * 128], gcol[:, 12 + h:13 + h], SC, ALU.mult, ALU.mult,
                       r=(kpsk, 'geb'), w=('kk',))
                for h in range(4):
                    act(vt[:, h, 0:128], psvm[:, h * 128:(h + 1) * 128], AF.Copy, r=(kpsvm, 'gea'), w=('vt',),
                        scale=gcol[:, 4 + h:5 + h])
                cp('dve', vt[:, :, 128], gcol[:, 4:8], r=('gea',), w=('vt',))
                if role == 'own':
                    tl = (t - NPRE) % 16
                    psq, kpsq = nbank()
                    pskT, kpskT = nbank()
                    for h in range(4):
                        mm(psq[:, h * 128:(h + 1) * 128], wq16[:, h, :], c16[:, h, :], True, True, r=('c16', 'wq16'), w=(kpsq,))
                    for h in range(4):
                        mm(pskT[:, h * 128:(h + 1) * 128], wk16[:, h, :], c16[:, h, :], True, True, r=('c16', 'wk16'), w=(kpskT,))
                    cp('act', qT16[:].rearrange("p a b -> p (a b)"), psq[:], r=(kpsq,), w=('qT16',))
                    act(kT16[:].rearrange("p a b -> p (a b)"), pskT[:], AF.Copy, r=(kpskT,), w=('kT16',), scale=SC)
                    psS, kpsS = nbank()
                    for h in range(4):
                        mm(psS[:, h * 128:(h + 1) * 128], kT16[:, h, :], qT16[:, h, :], True, True, r=('kT16', 'qT16'), w=(kpsS,))
                    tt('dve', sm[:], psS[:].rearrange("p (a b) -> p a b", a=4),
                       tri[:].unsqueeze(1).to_broadcast([128, 4, 128]), ALU.mult, r=(kpsS, 'tri'), w=('sm',))
                    psN = [nbank(), nbank()]
                    for h in range(4):
                        pn, kpn = psN[h // 2]
                        o = pn[:, (h % 2) * 129:(h % 2) * 129 + 129]
                        mm(o, sm[:, h, :], vt[:, h, :], True, False, r=('sm', 'vt'), w=(kpn,))
                        mm(o, qT16[:, h, :], Cb[:, h, :], False, True, r=('qT16', 'Cb'), w=(kpn,))
                    for hp in range(2):
                        pn, kpn = psN[hp]
                        tt('dve', pc[:, 2 * hp:2 * hp + 2], pn[:, 128:258:129], gcol[:, 8 + 2 * hp:10 + 2 * hp], ALU.mult,
                           r=(kpn, 'geb'), w=('pc_tden',))
                    act(pc[:, 4:8], pc[:, 0:4], AF.Abs, r=('pc_tden',), w=('pc_t2',))
                    ts('dve', pc[:, 4:8], pc[:, 4:8], 1.0, None, ALU.max, None, r=('pc_t2',), w=('pc_t2',))
                    recip(pc[:, 8:12], pc[:, 4:8], r=('pc_t2',), w=('pc_rt',))
                    tt('dve', pc[:, 12:16], pc[:, 8:12], gcol[:, 8:12], ALU.mult, r=('pc_rt', 'geb'), w=('pc_sc',))
                    for h in range(4):
                        pn, kpn = psN[h // 2]
                        act(junk[:, 0:128], pn[:, (h % 2) * 129:(h % 2) * 129 + 128], AF.Square, r=(kpn,),
                            w=('pc_ssn',) + (('junk',) if STRICT else ()), accum=pc[:, 16 + h:17 + h])
                    tt('dve', pc[:, 20:24], pc[:, 12:16], pc[:, 12:16], ALU.mult, r=('pc_sc',), w=('pc_v',))
                    tt('dve', pc[:, 20:24], pc[:, 20:24], pc[:, 16:20], ALU.mult, r=('pc_v', 'pc_ssn'), w=('pc_v',))
                    act(pc[:, 24:28], pc[:, 20:24], AF.Sqrt, r=('pc_v',), w=('pc_sq',), bias=EPS, scale=1.0 / 128)
                    recip(pc[:, 24:28], pc[:, 24:28], r=('pc_sq',), w=('pc_sq',))
                    tt('dve', pc[:, 28:32], pc[:, 24:28], pc[:, 12:16], ALU.mult, r=('pc_sq', 'pc_sc'), w=('pc_fac',))
                    for h in range(4):
                        pn, kpn = psN[h // 2]
                        act(hn[:, h, :], pn[:, (h % 2) * 129:(h % 2) * 129 + 128], AF.Copy, r=(kpn, 'pc_fac'), w=('hn',),
                            scale=pc[:, 28 + h:29 + h])
                    psH, kpsH = nbank()
                    for h in range(4):
                        tr(psH[:, h * 128:(h + 1) * 128], hn[:, h, :], idf[:], r=('hn', 'idf'), w=(kpsH,))
                    for h in range(4):
                        ts('dve', skc[:, h, :], skc[:, h, :], sk[:, h:h + 1], None, ALU.mult, None, r=('skc', 'sk_sb'), w=('skc',))
                    for h in range(4):
                        stt(skc[:, h, :], psH[:, h * 128:(h + 1) * 128], mg[:, h:h + 1], skc[:, h, :], ALU.mult, ALU.add,
                            r=(kpsH, 'mg_sb', 'skc'), w=('skc',))
                    tt('dve', ymT_mb[:, :, tl * 128:(tl + 1) * 128], skc[:], szb[:], ALU.mult, r=('skc', 'szb'), w=('ymT_mb',))
                psP = [nbank(), nbank()]
                for h in range(4):
                    pp, kpp = psP[h // 2]
                    mm(pp[:, (h % 2) * 129:(h % 2) * 129 + 129], kk[:, h, :], vt[:, h, :], True, True, r=('kk', 'vt'), w=(kpp,))
                for h in range(4):
                    pp, kpp = psP[h // 2]
                    stt(Cst[:, h, :], Cst[:, h, :], gcol[:, 12 + h:13 + h], pp[:, (h % 2) * 129:(h % 2) * 129 + 129],
                        ALU.mult, ALU.add, r=('Cst', 'geb', kpp), w=('Cst',))
                if t == NPRE - 1:
                    ts('dve', Cst[:], Cst[:], fl[:, 0:1], None, ALU.mult, None, r=('Cst', 'fl'), w=('Cst',))
                cp('act', Cb[:], Cst[:], r=('Cst',), w=('Cb',))

            vk = [sb(f"vk{i}", [128, 8, 65], BF16) for i in range(4)]
            sbb = [sb(f"sbb{i}", [128, 4, 128]) for i in range(2)]
            pTt = [sb(f"pTt{i}", [128, 4, 128], BF16) for i in range(2)]
            osb = [sb(f"osb{i}", [128, 8, 65]) for i in range(2)]
            att_ctr = [0]

            SBs = [sbb[0], sbb[1], hn, szb]
            PTs = [pTt[0], pTt[1], sm, c16]

            def attention(m):
                A0 = 2048 * (m + 1)
                its = []
                for pi, (win, d) in enumerate(PATS):
                    span = 128 * d
                    first = True
                    for blk in range(2048 // span):
                        for rr_ in range(d):
                            i = att_ctr[0]
                            att_ctr[0] += 1
                            qa0 = A0 + blk * span + rr_
                            for hg in range(2):
                                its.append(dict(pi=pi, d=d, span=span, qa0=qa0, ka=(qa0 - span, qa0),
                                                vks=(vk[(2 * i) % 4], vk[(2 * i + 1) % 4]), ql=qa0 - A0,
                                                O=osb[i % 2], hg=hg, newpat=first and hg == 0))
                            first = False

                def stage1(n, I):
                    d, span, ka, vks, ql, hg, pi = I['d'], I['span'], I['ka'], I['vks'], I['ql'], I['hg'], I['pi']
                    if I['newpat']:
                        dma('sp', tb[:], tbd[pi], r=(), w=('tb_sb',), stream='Ltb')
                    if hg == 0:
                        for j in range(2):
                            dma('act', vks[j][:].rearrange("p a b -> p (a b)"), vscr[ka[j]:ka[j] + span - d + 1:d, :],
                                r=('vscr',), w=(vks[j].name,), stream='L' + vks[j].name)
                    for j in range(2):
                        S_, PT_ = SBs[2 * (n % 2) + j], PTs[2 * (n % 2) + j]
                        kp = ((ka[j] // 2048) % 2) * 2048 + ka[j] % 2048
                        ps, kps = nbank()
                        for hh in range(4):
                            head = 2 * hh + hg
                            g, po = head // 2, (head % 2) * 64
                            mm(ps[:, hh * 128:(hh + 1) * 128], kT_all[po:po + 64, g, kp:kp + span - d + 1:d],
                               qT_mb[po:po + 64, g, ql:ql + span - d + 1:d], True, True, r=('kT_all', 'qT_mb'), w=(kps,))
                        c0 = 128 if j == 0 else 0
                        tbv = tb[:].rearrange("p (h c) -> p h c", h=8)[:, hg:8:2, c0:c0 + 128]
                        tt('dve', S_[:], ps[:].rearrange("p (a b) -> p a b", a=4), tbv, ALU.add,
                           r=(kps, 'tb_sb'), w=(S_.name,))
                        act(PT_[:], S_[:], AF.Exp, r=(S_.name,), w=(PT_.name,))

                def stage2(n, I):
                    d, span, vks, ql, hg, pi, O = I['d'], I['span'], I['vks'], I['ql'], I['hg'], I['pi'], I['O']
                    po_, kpo = nbank()
                    for hh in range(4):
                        head = 2 * hh + hg
                        for j in range(2):
                            PT_ = PTs[2 * (n % 2) + j]
                            mm(po_[:, hh * 65:hh * 65 + 65], PT_[:, hh, :], vks[j][:, head, :], j == 0, j == 1,
                               r=(PT_.name, vks[j].name), w=(kpo,))
                    cp('act' if hg else 'dve', O[:, hg:8:2, :],
                       po_[:, 0:260].rearrange("p (a b) -> p a b", a=4), r=(kpo,), w=(O.name,))
                    if hg == 1:
                        dma('sp', oscr[pi, ql + m * 2048:ql + m * 2048 + span - d + 1:d, :], O[:].rearrange("p a b -> p (a b)"),
                            r=(O.name,), w=('oscr',), stream='S' + O.name)

                stage1(0, its[0])
                for n in range(len(its)):
                    if n + 1 < len(its):
                        stage1(n + 1, its[n + 1])
                    stage2(n, its[n])

            o3 = [[sb(f"o3_{p}", [128, 8, 65]) for p in range(3)]] * 2
            ya16 = sb("ya16", [128, 8, 64], BF16)
            rden = sb("rden", [128, 8])
            yaT = sb("yaT", [128, 4, 128], BF16)

            def combine(m, tl):
                to = m * 16 + tl
                b = tl % 2
                os_ = o3[b]
                for p in range(3):
                    dma('act', os_[p][:].rearrange("p a b -> p (a b)"), oscr[p, to * 128:(to + 1) * 128, :],
                        r=('oscr',), w=(os_[p].name,), stream='L' + os_[p].name)
                X = xt[b]
                dma('sp', X[:], xall[(NPRE + to) * 128:(NPRE + to + 1) * 128, :], r=(), w=(X.name,), stream='L' + X.name)
                tt('dve', os_[0][:], os_[0][:], os_[1][:], ALU.add, r=(os_[0].name, os_[1].name), w=(os_[0].name,))
                tt('dve', os_[0][:], os_[0][:], os_[2][:], ALU.add, r=(os_[0].name, os_[2].name), w=(os_[0].name,))
                recip(rden[:], os_[0][:, :, 64], r=(os_[0].name,), w=('rden',))
                tt('dve', ya16[:], os_[0][:, :, 0:64], rden[:].unsqueeze(2).to_broadcast([128, 8, 64]), ALU.mult,
                   r=(os_[0].name, 'rden'), w=('ya16',))
                pT, kpT = nbank()
                pTb = pT[:].bitcast(BF16)
                yaf = ya16[:].rearrange("p a b -> p (a b)")
                for fc in range(4):
                    tr(pTb[:, fc * 128:(fc + 1) * 128], yaf[:, fc * 128:(fc + 1) * 128], idb[:], r=('ya16', 'idb'), w=(kpT,))
                cp('act', yaT[:].rearrange("p a b -> p (a b)"), pTb[:, 0:512], r=(kpT,), w=('yaT',))
                X1 = X
                for nh in range(2):
                    py, kpy = nbank()
                    for kc in range(8):
                        lhs = ymT_mb[:, kc, tl * 128:(tl + 1) * 128] if kc < 4 else yaT[:, kc - 4, :]
                        mm(py[:], lhs, w_out_sb[:, kc, nh * 512:(nh + 1) * 512], kc == 0, kc == 7,
                           r=('ymT_mb', 'yaT', 'w_out_sb0', 'w_out_sb1'), w=(kpy,))
                    tt('dve', X1[:, nh * 512:(nh + 1) * 512], py[:], X[:, nh * 512:(nh + 1) * 512], ALU.add,
                       r=(kpy, X.name), w=(X.name,))
                dma('sp', outd[to * 128:(to + 1) * 128, :], X1[:], r=(X1.name,), w=('outd',), stream='S' + X1.name)

            cst = sb("cst", [128, 512])
            cst16 = sb("cst16", [128, 512], BF16)
            conv_i = [0]
            NCH = 512

            def convert(n):
                for _ in range(n):
                    i = conv_i[0]
                    if i >= NCH:
                        return
                    conv_i[0] += 1
                    rb_, cb_ = i // 4, i % 4
                    dma('pool', cst[:], exuv[rb_ * 128:(rb_ + 1) * 128, cb_ * 512:(cb_ + 1) * 512], r=(), w=('cst',), stream='Lcst')
                    cp('pool', cst16[:], cst[:], r=('cst',), w=('cst16',))
                    dma('pool', exb[rb_ * 128:(rb_ + 1) * 128, cb_ * 512:(cb_ + 1) * 512], cst16[:], r=('cst16',), w=(), stream='Scst16')
            per_tile = -(-NCH // NT)
            _tileA = tileA

            def tileA(t):
                convert(per_tile)
                _tileA(t)
            import os
            KS = os.environ.get("KSTOP", "")
            if KS == "setup":
                pass
            elif KS == "t1":
                tileA(0)
            elif KS == "pre":
                for t in range(NPRE):
                    tileA(t)
            elif KS == "own1":
                for t in range(NPRE + 1):
                    tileA(t)
            elif KS == "att":
                for t in range(NPRE + 16):
                    tileA(t)
                attention(0)
            else:
                for t in range(NPRE):
                    tileA(t)
                for m in range(NOWN // 16):
                    for tl in range(16):
                        tileA(NPRE + m * 16 + tl)
                    attention(m)
                    for tl in range(16):
                        combine(m, tl)
                convert(NCH)
            nsA = P.emit(es)
        if with_peer:
            esB = ExitStack()
            with esB:
                buildB(nc, es, esB, NOWN, outd, g2d, w_query, skeys, c_iota, c_idf, exb)
    return nc


def buildB(nc, es, esB, NOWN, outd, g2d, w_query, skeys, c_iota, c_idf, exb):
    P = Prog(nc, "B")
    NG, NDG, GS = 24, 8, 4

    def sb(name, shape, dt=F32):
        return esB.enter_context(nc.sbuf_tensor(name, list(shape), dt))

    rbanks = [esB.enter_context(nc.psum_tensor(f"rb{i}", [128, 512], F32)) for i in range(4)]
    h2f = esB.enter_context(nc.psum_tensor("h2f", [128, D], F32))
    pout = [esB.enter_context(nc.psum_tensor(f"pout{i}", [128, 512], F32)) for i in range(2)]
    bank_i = [0]

    def nbank():
        b = bank_i[0] % 4
        bank_i[0] += 1
        return rbanks[b], ('rb', b)

    def dma(q, out, in_, r, w, stream, final=False):
        P.add(q, lambda e: e.dma_start(out=out, in_=in_), r=r, w=w, stream=stream, final=final)

    def mm(out, lhsT, rhs, start, stop, r, w):
        P.add('pe', lambda e: e.matmul(out, lhsT, rhs, start=start, stop=stop), r=r, w=w)

    def tr(out, in_, ident, r, w):
        P.add('pe', lambda e: e.transpose(out, in_, ident), r=r, w=w)

    def act(out, in_, func, r, w, bias=None, scale=None, accum=None):
        kw = {}
        if bias is not None:
            kw['bias'] = bias
        if scale is not None:
            kw['scale'] = scale
        if accum is not None:
            kw['accum_out'] = accum
        P.add('act', lambda e: e.activation(out, in_, func, **kw), r=r, w=w)

    def tt(eng, out, in0, in1, op, r, w):
        P.add(eng, lambda e: e.tensor_tensor(out, in0, in1, op), r=r, w=w)

    def ts(eng, out, in0, s1, s2, op0, op1, r, w):
        if op1 is None:
            P.add(eng, lambda e: e.tensor_scalar(out, in0, s1, None, op0), r=r, w=w)
        else:
            P.add(eng, lambda e: e.tensor_scalar(out, in0, s1, s2, op0, op1), r=r, w=w)

    def stt(out, in0, scalar, in1, op0, op1, r, w, accum=None):
        P.add('dve', lambda e: e.scalar_tensor_tensor(out, in0, scalar, in1, op0, op1, accum_out=accum), r=r, w=w)

    def cp(eng, out, in_, r, w):
        if eng == 'act':
            P.add('act', lambda e: e.copy(out, in_), r=r, w=w)
        else:
            P.add(eng, lambda e: e.tensor_copy(out, in_), r=r, w=w)

    def recip(out, in_, r, w):
        P.add('dve', lambda e: e.reciprocal(out, in_), r=r, w=w)

    def red(out, in_, op, r, w):
        P.add('dve', lambda e: e.tensor_reduce(out, in_, AX.X, op), r=r, w=w)

    wq_sb = sb("wq_sb", [128, 8, 2048], BF16)
    skT = sb("skT", [128, 16, 128], BF16)
    idf = sb("idfB", [128, 128])
    idb = sb("idbB", [128, 128], BF16)
    g2b = sb("g2b", [128, D])
    iot = sb("iot", [128, 16])
    x1t = [sb(f"x1t{i}", [128, D]) for i in range(3)]
    hb2 = sb("hb2", [128, D], BF16)
    h2T = sb("h2T", [128, 8, 128], BF16)
    junk = sb("junkB", [128, D], BF16)
    junkD = sb("junkD", [128, D], BF16)
    ssq = [sb(f"ssqB{i}", [128, 4]) for i in range(2)]
    qpT = sb("qpT", [128, 16, 128], BF16)
    sc = sb("sc", [128, 16, 128])
    sc2 = sb("sc2", [128, 16, 128])
    v16 = sb("v16", [128, 16, 16])
    i16 = sb("i16", [128, 16, 16], U32)
    i16f = sb("i16f", [128, 16, 16])
    cand = sb("cand", [128, 8, 256])
    cand2 = sc[:].rearrange("p (h a) k -> p h (a k)", h=8)
    vc = sb("vc", [128, 8, 16])
    ic = sb("ic", [128, 8, 16], U32)
    ich = sb("ich", [128, 8, 16], U32)
    icl = sb("icl", [128, 8, 16], U32)
    ichf = sb("ichf", [128, 8, 16])
    iclf = sb("iclf", [128, 8, 16])
    eq = sc2[:].rearrange("p (h a) (b c) -> p h (a b) c", h=8, c=16)
    e12 = sb("e12", [128, 2, 8, 16])
    idxf = sb("idxf", [128, 128])
    idx = [sb(f"idx{i}", [128, 128], I32) for i in range(2)]
    gate = [sb(f"gate{i}", [128, 128]) for i in range(2)]
    gtmp = sb("gtmp", [128, 8, 16])
    gs = sb("gs", [128, 16])
    pre = sb("pre", [128, 128])
    av = sb("av", [128, 128])
    dg = [sb(f"dg{i}", [128, 128], BF16) for i in range(NDG)]
    gbuf = [sb(f"gb{i}", [128, 2 * D], BF16) for i in range(NG)]
    osb = [sb(f"osbB{i}", [128, D]) for i in range(2)]

    SU = 'setupB'
    dma('sp', idf[:], c_idf, r=(), w=('idfB',), stream=SU, final=True)
    dma('sp', g2b[:], g2d.partition_broadcast(128), r=(), w=('g2b',), stream=SU, final=True)
    dma('sp', iot[:], c_iota.partition_broadcast(128), r=(), w=('iot',), stream=SU, final=True)
    cp('dve', idb[:], idf[:], r=('idfB',), w=('idbB',))
    si = 0
    for kc in range(8):
        for c0 in range(0, 2048, 1024):
            s_ = x1t[si % 2]
            dma('sp', s_[:], w_query[kc * 128:(kc + 1) * 128, c0:c0 + 1024], r=(), w=(s_.name,), stream='L' + s_.name)
            cp('act' if si % 2 else 'dve', wq_sb[:, kc, c0:c0 + 1024], s_[:], r=(s_.name,), w=('wq_sb%d' % (si % 2),))
            si += 1
    for q4 in range(4):
        s_ = x1t[si % 2]
        dma('sp', s_[:, 0:512].rearrange("p (a b) -> p a b", a=4), skeys[q4 * 4:(q4 + 1) * 4].rearrange("a k c -> k a c"),
            r=(), w=(s_.name,), stream='L' + s_.name)
        ps, kps = nbank()
        for a in range(4):
            tr(ps[:, a * 128:(a + 1) * 128], s_[:, a * 128:(a + 1) * 128], idf[:], r=(s_.name, 'idfB'), w=(kps,))
        cp('act', skT[:, q4 * 4:(q4 + 1) * 4, :].rearrange("p a b -> p (a b)"), ps[:], r=(kps,), w=('skT',))
        si += 1

    def FE(t):
        X = x1t[t % 3]
        SS = ssq[t % 2]
        IDX = idx[t % 2]
        G = gate[t % 2]
        dma('sp', X[:], outd[t * 128:(t + 1) * 128, :], r=(('outd', t),), w=(X.name,), stream='L' + X.name)
        act(junk[:], X[:], AF.Square, r=(X.name,), w=(SS.name + 'a',) + (('junkB',) if STRICT else ()), accum=SS[:, 0:1])
        act(SS[:, 1:2], SS[:, 0:1], AF.Sqrt, r=(SS.name + 'a',), w=(SS.name + 'b',), bias=EPS, scale=1.0 / D)
        yield
        recip(SS[:, 2:3], SS[:, 1:2], r=(SS.name + 'b',), w=(SS.name,))
        stt(hb2[:], X[:], SS[:, 2:3], g2b[:], ALU.mult, ALU.mult, r=(X.name, SS.name, 'g2b'), w=('hb2',))
        pT, kpT = nbank()
        pTb = pT[:].bitcast(BF16)
        for kc in range(8):
            tr(pTb[:, kc * 128:(kc + 1) * 128], hb2[:, kc * 128:(kc + 1) * 128], idb[:], r=('hb2', 'idbB'), w=(kpT,))
        cp('act', h2T[:].rearrange("p a b -> p (a b)"), pTb[:, 0:1024], r=(kpT,), w=('h2T',))
        yield
        for q4 in range(4):
            ps, kps = nbank()
            for a in range(4):
                blk = q4 * 4 + a
                for kc in range(8):
                    mm(ps[:, a * 128:(a + 1) * 128], wq_sb[:, kc, blk * 128:(blk + 1) * 128], h2T[:, kc, :], kc == 0, kc == 7,
                       r=('wq_sb0', 'wq_sb1', 'h2T'), w=(kps,))
            cp('act', qpT[:, q4 * 4:(q4 + 1) * 4, :].rearrange("p a b -> p (a b)"), ps[:], r=(kps,), w=('qpT',))
            yield
        for q4 in range(4):
            ps, kps = nbank()
            for a in range(4):
                blk = q4 * 4 + a
                mm(ps[:, a * 128:(a + 1) * 128], qpT[:, blk, :], skT[:, blk, :], True, True, r=('qpT', 'skT'), w=(kps,))
            cp('act', sc[:, q4 * 4:(q4 + 1) * 4, :].rearrange("p a b -> p (a b)"), ps[:], r=(kps,), w=('sc',))
            yield

        def top16_ops(src, src2, vout, iout, key, key2, tag):
            kv, ki, k2 = (key + 'v', tag), (key + 'i', tag), (key2, tag)
            return [
                (lambda e: e.max(vout[:, 0:8], src), (key,), (kv,)),
                (lambda e: e.max_index(iout[:, 0:8], vout[:, 0:8], src), (key, kv), (ki,)),
                (lambda e: e.match_replace(src2, vout[:, 0:8], src, -1e30), (key, kv), (k2,)),
                (lambda e: e.max(vout[:, 8:16], src2), (k2,), (kv,)),
                (lambda e: e.max_index(iout[:, 8:16], vout[:, 8:16], src2), (k2, kv), (ki,)),
            ]

        def emit_interleaved(lists):
            for grp_ops in zip(*lists):
                for fn, r_, w_ in grp_ops:
                    P.add('dve', fn, r=r_, w=w_)
                yield

        for b0 in range(0, 16, 8):
            yield from emit_interleaved([top16_ops(sc[:, blk, :], sc2[:, blk, :], v16[:, blk, :], i16[:, blk, :], 'sc', 'sc2', blk)
                                         for blk in range(b0, b0 + 8)])
        SCV = tuple(('scv', blk) for blk in range(16))
        SCI = tuple(('sci', blk) for blk in range(16))
        SC2 = tuple(('sc2', blk) for blk in range(16))
        EQK = ('sc2',) + SC2
        cp('dve', i16f[:], i16[:], r=SCI, w=('i16f',))
        v4 = v16[:].rearrange("p (h two) s -> p h two s", two=2)
        tt('dve', cand[:].rearrange("p h (i j) -> p h i j", i=16),
           v4[:, :, 0, :].unsqueeze(3).to_broadcast([128, 8, 16, 16]),
           v4[:, :, 1, :].unsqueeze(2).to_broadcast([128, 8, 16, 16]), ALU.add, r=SCV, w=('cand',))
        for h0 in range(0, 8, 8):
            lists = [top16_ops(cand[:, h, :], cand2[:, h, :], vc[:, h, :], ic[:, h, :], 'cand', 'scA', h) for h in range(h0, h0 + 8)]
            for ops in lists:
                fn, r_, w_ = ops[2]
                ops[2] = (fn, r_, w_ + ('sc',))
                for q in (3, 4):
                    fn, r_, w_ = ops[q]
                    ops[q] = (fn, r_ + ('sc',), w_)
            yield from emit_interleaved(lists)
        CV = tuple(('candv', h) for h in range(8))
        CI = tuple(('candi', h) for h in range(8))
        P.add('dve', lambda e: e.tensor_single_scalar(ich[:], ic[:], 4, ALU.logical_shift_right), r=CI, w=('ich',))
        P.add('dve', lambda e: e.tensor_single_scalar(icl[:], ic[:], 15, ALU.bitwise_and), r=CI, w=('icl',))
        cp('dve', ichf[:], ich[:], r=('ich',), w=('ichf',))
        cp('dve', iclf[:], icl[:], r=('icl',), w=('iclf',))
        i4 = i16f[:].rearrange("p (h two) s -> p h two s", two=2)
        iob = iot[:].unsqueeze(1).unsqueeze(1).to_broadcast([128, 8, 16, 16])
        for w_, srcf in ((0, ichf), (1, iclf)):
            tt('dve', eq, srcf[:].unsqueeze(3).to_broadcast([128, 8, 16, 16]), iob, ALU.is_equal,
               r=(srcf.name, 'iot'), w=EQK)
            tt('dve', eq, eq, i4[:, :, w_, :].unsqueeze(2).to_broadcast([128, 8, 16, 16]), ALU.mult,
               r=EQK + ('i16f',), w=EQK)
            red(e12[:, w_, :, :], eq, ALU.add, r=EQK, w=('e12',))
            yield
        stt(idxf[:], e12[:, 0, :, :].rearrange("p a b -> p (a b)"), 128.0, e12[:, 1, :, :].rearrange("p a b -> p (a b)"),
            ALU.mult, ALU.add, r=('e12',), w=('idxf',))
        cp('dve', IDX[:], idxf[:], r=('idxf',), w=(IDX.name,))
        tt('dve', gtmp[:], vc[:], vc[:, :, 0:1].to_broadcast([128, 8, 16]), ALU.subtract, r=CV, w=('gtmp',))
        act(gtmp[:], gtmp[:], AF.Exp, r=('gtmp',), w=('gtmp',))
        yield
        red(gs[:, 0:8], gtmp[:], ALU.add, r=('gtmp',), w=('gs',))
        recip(gs[:, 8:16], gs[:, 0:8], r=('gs',), w=('gs',))
        tt('dve', G[:].rearrange("p (a b) -> p a b", a=8), gtmp[:], gs[:, 8:16].unsqueeze(2).to_broadcast([128, 8, 16]),
           ALU.mult, r=('gtmp', 'gs'), w=(G.name,))

    gctr = [0]

    def BE(t, fe=None):
        X = x1t[t % 3]
        SS = ssq[t % 2]
        IDX = idx[t % 2]
        G = gate[t % 2]
        stt(h2f[:], X[:], SS[:, 2:3], g2b[:], ALU.mult, ALU.mult, r=(X.name, SS.name, 'g2b'), w=('h2f',))
        ngrp = 128 // GS

        def front(grp):
            bufs = []
            for s_ in range(GS):
                k = grp * GS + s_
                c = gctr[0]
                gctr[0] += 1
                B_ = gbuf[c % NG]
                bufs.append(B_)
                P.add('pool', lambda e, B_=B_, k=k: e.indirect_dma_start(
                    out=B_[:, :], out_offset=None, in_=exb[:, :],
                    in_offset=bass.IndirectOffsetOnAxis(ap=IDX[:, k:k + 1], axis=0)),
                    r=(IDX.name,), w=(B_.name,), stream='G' + B_.name)
                if STRICT:
                    stt(B_[:, 0:D], B_[:, 0:D], 1.0, h2f[:], ALU.mult, ALU.mult, r=(B_.name, 'h2f'),
                        w=(('pre', grp, s_), B_.name), accum=pre[:, k:k + 1])
                else:
                    stt(junkD[:], B_[:, 0:D], 1.0, h2f[:], ALU.mult, ALU.mult, r=(B_.name, 'h2f'), w=(('pre', grp, s_),),
                        accum=pre[:, k:k + 1])
            return bufs

        def back(grp, bufs):
            g0 = grp * GS
            act(av[:, g0:g0 + GS], pre[:, g0:g0 + GS], AF.Gelu, r=tuple(('pre', grp, q) for q in range(GS)), w=(('av', grp),))
            tt('dve', av[:, g0:g0 + GS], av[:, g0:g0 + GS], G[:, g0:g0 + GS], ALU.mult, r=(('av', grp), G.name), w=(('av', grp),))
            for s_ in range(GS):
                k = g0 + s_
                B_ = bufs[s_]
                DG = dg[k % NDG]
                act(DG[:], idf[:], AF.Copy, r=('idfB', ('av', grp)), w=(DG.name,), scale=av[:, k:k + 1])
                for nh in range(2):
                    mm(pout[nh][:], DG[:], B_[:, D + nh * 512:D + (nh + 1) * 512], k == 0, k == 127,
                       r=(DG.name, B_.name), w=('pout%d' % nh,))

        pend = []
        for grp in range(ngrp):
            pend.append((grp, front(grp)))
            if fe is not None:
                for _ in range(2):
                    next(fe, None)
            if len(pend) > 2:
                back(*pend.pop(0))
        for p_ in pend:
            back(*p_)
        if fe is not None:
            for _ in fe:
                pass
        O = osb[t % 2]
        for nh in range(2):
            tt('dve', O[:, nh * 512:(nh + 1) * 512], pout[nh][:], X[:, nh * 512:(nh + 1) * 512], ALU.add,
               r=('pout%d' % nh, X.name), w=(O.name,))
        dma('sp', outd[t * 128:(t + 1) * 128, :], O[:], r=(O.name,), w=(('outd', t),), stream='S' + O.name)

    for _ in FE(0):
        pass
    for t in range(NOWN):
        BE(t, FE(t + 1) if t + 1 < NOWN else None)
    P.emit(es)


def _bucket_table(rel_bias):
    nb, maxd = 32, 2048
    me = nb // 2
    out = np.full((3, 128, 8, 256), NEG, np.float32)
    j = np.arange(128)[:, None]
    c = np.arange(256)[None, :]
    steps = c - j
    valid = (steps >= 0) & (steps <= 128)
    for pi, (win, d) in enumerate(PATS):
        dist = np.maximum(steps, 0) * d
        nf = np.maximum(dist, 1).astype(np.float32)
        large = me + (np.log(nf / me) / np.log(maxd / me) * (nb - me)).astype(np.int32)
        large = np.minimum(large, nb - 1)
        bidx = np.where(dist < me, dist, large)
        for h in range(8):
            out[pi, :, h, :] = np.where(valid, rel_bias[bidx, h], NEG)
    return out.reshape(3, 128, 8 * 256)


def make_in_maps(inp, cores, NPRE, NOWN):
    f = lambda a: np.ascontiguousarray(a, dtype=np.float32)
    ii = np.arange(128)
    common = {
        "w_in": f(inp['w_in'][0]),
        "g1c": f(inp['norm1_g'][0].reshape(8, 128).T),
        "cw": f(inp['conv_w'][0].reshape(4, 4, 128).transpose(2, 1, 0).reshape(128, 16)),
        "cb": f(inp['conv_b'][0].reshape(4, 128).T),
        "wqm": f(inp['wq_m'][0].transpose(1, 0, 2)),
        "wkm": f(inp['wk_m'][0].transpose(1, 0, 2)),
        "gb": f(np.concatenate([inp['ig_b'][0], inp['fg_b'][0]])[None, :]),
        "mg": f(inp['mh_norm_g'][0].T),
        "sk": f(inp['skip_m'][0].T),
        "qg": f(inp['qn_g'][0].reshape(4, 128).T),
        "kg": f(inp['kn_g'][0].reshape(4, 128).T),
        "tb": _bucket_table(np.asarray(inp['rel_bias'], np.float32)),
        "w_out": f(inp['w_out'][0]),
        "c_idf": f(np.eye(128)),
        "c_tri": f(ii[:, None] <= ii[None, :]),
        "c_one": f(np.ones((128, 128))),
        "c_blk": f((ii[:, None] // 64) == (ii[None, :] // 64)),
        "g2": f(inp['norm2_g'][0][None, :]),
        "w_query": f(inp['w_query'][0]),
        "skeys": f(np.stack([inp['sub_keys1'][0], inp['sub_keys2'][0]], axis=1).reshape(16, 128, 128)),
        "c_iota": f(np.arange(16)[None, :]),
        "exuv": f(np.concatenate([inp['expert_u'][0], inp['expert_v'][0]], axis=1)),
    }
    maps = []
    for xp, xo, flag in cores:
        m = dict(common)
        m["xall"] = f(np.concatenate([xp, xo], axis=0))
        m["flg"] = np.full((128, 2), float(flag), np.float32)
        maps.append(m)
    return maps


def kernel(**inp):
    inp = {k: np.asarray(v) for k, v in inp.items()}
    x = inp['x']
    B, S, _ = x.shape
    NPRE, NOWN = 32, 32
    half = NOWN * 128
    cores = []
    for b in range(B):
        cores.append((np.zeros((NPRE * 128, D), np.float32), x[b, 0:half], 0.0))
        cores.append((x[b, 0:half], x[b, half:2 * half], 1.0))
    nc = build(NPRE, NOWN, with_peer=True)
    in_maps = make_in_maps(inp, cores, NPRE, NOWN)
    res = run_bass_kernel_spmd(nc, in_maps, core_ids=list(range(len(cores))))
    out = np.zeros((B, S, D), np.float32)
    for c in range(len(cores)):
        b, hf = c // 2, c % 2
        out[b, hf * half:(hf + 1) * half] = res.results[c]["out"]
    return out
```

```python
import numpy as np
from contextlib import ExitStack
import concourse.bass as bass
import concourse.mybir as mybir
from concourse.bass_utils import run_bass_kernel_spmd

F32 = mybir.dt.float32
BF16 = mybir.dt.bfloat16
I32 = mybir.dt.int32
U32 = mybir.dt.uint32
AF = mybir.ActivationFunctionType
ALU = mybir.AluOpType
AX = mybir.AxisListType

D = 1024
INC = 3080
EPS = 1e-6
NEG = -30000.0
PATS = ((128, 1), (512, 4), (2048, 16))
W_ROT = 30000
ENGS = ['pe', 'act', 'dve', 'pool', 'sp']


class Prog:
    def __init__(self, nc, name):
        self.nc = nc
        self.name = name
        self.ops = {e: [] for e in ENGS}
        self.cnt = {e: 0 for e in ENGS}
        self.lastw = {}
        self.readers = {}
        self.waited = {e: {} for e in ENGS}
        self.scnt = {}
        self.final_streams = set()

    def add(self, e, fn, r=(), w=(), stream=None, final=False):
        deps = {}
        def dep(tok):
            if tok is None:
                return
            sid, v = tok
            if v is None:
                deps[sid] = None
            elif sid not in deps or (deps[sid] is not None and deps[sid] < v):
                deps[sid] = v
        def same_eng(tok):
            return tok is not None and tok[0][0] == 'E' and tok[0][1] == e
        for k in r:
            dep(self.lastw.get(k))
        for k in w:
            t0 = self.lastw.get(k)
            if not same_eng(t0):
                dep(t0)
            for t in self.readers.get(k, ()):
                if not same_eng(t):
                    dep(t)
        if stream is None:
            idx = self.cnt[e]
            self.cnt[e] += 1
            j = idx // W_ROT
            tok = (('E', e, j), idx - j * W_ROT + 1)
            inc = (tok[0], 1)
        else:
            n = self.scnt.get(stream, 0) + 1
            self.scnt[stream] = n
            if final:
                self.final_streams.add(stream)
                tok = (('D', stream), None)
            else:
                tok = (('D', stream), 16 * n)
            inc = (('D', stream), 16)
        waits = []
        for sid, v in deps.items():
            if e == 'pe' and sid[0] == 'E' and sid[1] == 'pe':
                continue
            if v is None:
                if self.waited[e].get(sid) == 'F':
                    continue
                self.waited[e][sid] = 'F'
                waits.append((sid, None))
                continue
            pv = self.waited[e].get(sid, 0)
            if pv == 'F' or pv >= v:
                continue
            self.waited[e][sid] = v
            waits.append((sid, v))
        self.ops[e].append((waits, fn, inc))
        for k in w:
            self.lastw[k] = tok
            self.readers[k] = []
        for k in r:
            if k not in w:
                self.readers.setdefault(k, []).append(tok)
        return tok

    def emit(self, es, final_wait_engine='sp'):
        nc = self.nc
        sems = {}

        def get(sid):
            if sid not in sems:
                nm = self.name + '_' + '_'.join(str(s) for s in sid)
                sems[sid] = es.enter_context(nc.semaphore(nm.replace(':', '_').replace('-', 'm')))
            return sems[sid]

        fin = []
        for st, n in self.scnt.items():
            fin.append((('D', st), 16 * n))
        self.ops[final_wait_engine].append((fin, None, None))
        with nc.Block() as block:
            decos = {'pe': block.tensor, 'act': block.scalar, 'dve': block.vector,
                     'pool': block.gpsimd, 'sp': block.sync}
            for e in ENGS:
                ops = self.ops[e]

                def body(eng, ops=ops):
                    for waits, fn, inc in ops:
                        for sid, v in waits:
                            if v is None:
                                v = 16 * self.scnt[sid[1]]
                            eng.wait_ge(get(sid), v)
                        if fn is None:
                            continue
                        ins = fn(eng)
                        ins.then_inc(get(inc[0]), inc[1])
                decos[e](body)
        return len(sems)


def build(NPRE, NOWN, with_peer=True):
    NT = NPRE + NOWN
    NHALO = 16
    assert NPRE >= NHALO and NPRE % 16 == 0 and NOWN % 16 == 0
    T0H = NPRE - NHALO
    nc = bass.Bass("TRN2", target_bir_lowering=False)

    def din(name, shape, dt=F32):
        return nc.dram_tensor(name, list(shape), dt, kind="ExternalInput").ap()

    xall = din("xall", [NT * 128, D])
    flg = din("flg", [128, 2])
    w_in = din("w_in", [D, INC])
    g1c = din("g1c", [128, 8])
    cwd = din("cw", [128, 16])
    cbd = din("cb", [128, 4])
    wqm = din("wqm", [128, 4, 128])
    wkm = din("wkm", [128, 4, 128])
    gbd = din("gb", [1, 8])
    mgd = din("mg", [128, 4])
    skd = din("sk", [128, 4])
    qgd = din("qg", [128, 4])
    kgd = din("kg", [128, 4])
    tbd = din("tb", [3, 128, 8 * 256])
    w_out = din("w_out", [D, D])
    c_idf = din("c_idf", [128, 128])
    c_tri = din("c_tri", [128, 128])
    c_one = din("c_one", [128, 128])
    c_blk = din("c_blk", [128, 128])
    g2d = din("g2", [1, D])
    w_query = din("w_query", [D, 2048])
    skeys = din("skeys", [16, 128, 128])
    c_iota = din("c_iota", [1, 16])
    exuv = din("exuv", [16384, 2 * D])
    exb = nc.dram_tensor("exb", [16384, 2 * D], BF16, kind="Internal").ap()
    outd = nc.dram_tensor("out", [NOWN * 128, D], F32, kind="ExternalOutput").ap()
    vscr = nc.dram_tensor("vscr", [(NHALO + NOWN) * 128, 520], BF16, kind="Internal").ap()
    oscr = nc.dram_tensor("oscr", [3, NOWN * 128, 520], F32, kind="Internal").ap()

    es = ExitStack()
    with es:
        esA = ExitStack()
        with esA:
            P = Prog(nc, "A")
            banks = [esA.enter_context(nc.psum_tensor(f"bank{i}", [128, 512], F32)) for i in range(8)]
            bank_i = [0]

            def nbank():
                b = bank_i[0] % 8
                bank_i[0] += 1
                return banks[b], ('bank', b)

            def sb(name, shape, dt=F32):
                return esA.enter_context(nc.sbuf_tensor(name, list(shape), dt))

            def dma(q, out, in_, r, w, stream, final=False):
                P.add(q, lambda e: e.dma_start(out=out, in_=in_), r=r, w=w, stream=stream, final=final)

            def mm(out, lhsT, rhs, start, stop, r, w):
                P.add('pe', lambda e: e.matmul(out, lhsT, rhs, start=start, stop=stop), r=r, w=w)

            def tr(out, in_, ident, r, w):
                P.add('pe', lambda e: e.transpose(out, in_, ident), r=r, w=w)

            def act(out, in_, func, r, w, bias=None, scale=None, accum=None, eng='act'):
                kw = {}
                if bias is not None:
                    kw['bias'] = bias
                if scale is not None:
                    kw['scale'] = scale
                if accum is not None:
                    kw['accum_out'] = accum
                P.add(eng, lambda e: e.activation(out, in_, func, **kw), r=r, w=w)

            def tt(eng, out, in0, in1, op, r, w):
                P.add(eng, lambda e: e.tensor_tensor(out, in0, in1, op), r=r, w=w)

            def ts(eng, out, in0, s1, s2, op0, op1, r, w):
                if op1 is None:
                    P.add(eng, lambda e: e.tensor_scalar(out, in0, s1, None, op0), r=r, w=w)
                else:
                    P.add(eng, lambda e: e.tensor_scalar(out, in0, s1, s2, op0, op1), r=r, w=w)

            def stt(out, in0, scalar, in1, op0, op1, r, w, accum=None):
                P.add('dve', lambda e: e.scalar_tensor_tensor(out, in0, scalar, in1, op0, op1, accum_out=accum), r=r, w=w)

            def cp(eng, out, in_, r, w):
                if eng == 'act':
                    P.add('act', lambda e: e.copy(out, in_), r=r, w=w)
                else:
                    P.add(eng, lambda e: e.tensor_copy(out, in_), r=r, w=w)

            def recip(out, in_, r, w):
                P.add('dve', lambda e: e.reciprocal(out, in_), r=r, w=w)

            def memset(eng, ap, val, w):
                P.add(eng, lambda e: e.memset(ap, val), w=w)

            w_in_sb = sb("w_in_sb", [128, 8, INC], BF16)
            w_out_sb = sb("w_out_sb", [128, 8, D], BF16)
            g1 = sb("g1", [128, 8])
            cw = sb("cw_sb", [128, 16])
            cb = sb("cb_sb", [128, 4])
            wq16 = sb("wq16", [128, 4, 128], BF16)
            wk16 = sb("wk16", [128, 4, 128], BF16)
            gb = sb("gb_sb", [128, 8])
            mg = sb("mg_sb", [128, 4])
            sk = sb("sk_sb", [128, 4])
            qg = sb("qg_sb", [128, 4])
            kg = sb("kg_sb", [128, 4])
            tb = sb("tb_sb", [128, 8 * 256])
            idf = sb("idf", [128, 128])
            idb = sb("idb", [128, 128], BF16)
            tri = sb("tri", [128, 128])
            one = sb("one", [128, 128])
            blk16 = sb("blk16", [128, 128], BF16)
            blkf = sb("blkf", [128, 128])
            fl = sb("fl", [128, 2])
            fl16 = sb("fl16", [128, 8], BF16)
            kT_all = sb("kT_all", [128, 4, 2 * 2048], BF16)
            qT_mb = sb("qT_mb", [128, 4, 2048], BF16)
            ymT_mb = sb("ymT_mb", [128, 4, 2048], BF16)
            Cst = sb("Cst", [128, 4, 129])
            Cb = sb("Cb", [128, 4, 129], BF16)
            uT = sb("uT", [128, 4, 131])
            junk = sb("junk", [128, 128], BF16)

            SU = 'setup'
            for (dst, src) in ((g1, g1c), (cw, cwd), (cb, cbd), (mg, mgd), (sk, skd), (qg, qgd), (kg, kgd),
                               (idf, c_idf), (tri, c_tri), (one, c_one), (blkf, c_blk), (fl, flg)):
                dma('sp', dst[:], src, r=(), w=(dst.name,), stream=SU, final=True)
            dma('sp', gb[:], gbd.partition_broadcast(128), r=(), w=('gb_sb',), stream=SU, final=True)
            xt = [sb(f"xt{i}", [128, D]) for i in range(2)]
            dma('sp', xt[0][:, 0:512], wqm.rearrange("p a b -> p (a b)"), r=(), w=('xt0',), stream='Lxt0')
            dma('sp', xt[1][:, 0:512], wkm.rearrange("p a b -> p (a b)"), r=(), w=('xt1',), stream='Lxt1')
            cp('dve', wq16[:].rearrange("p a b -> p (a b)"), xt[0][:, 0:512], r=('xt0',), w=('wq16',))
            cp('dve', wk16[:].rearrange("p a b -> p (a b)"), xt[1][:, 0:512], r=('xt1',), w=('wk16',))
            cp('dve', idb[:], idf[:], r=('idf',), w=('idb',))
            cp('dve', blk16[:], blkf[:], r=('blkf',), w=('blk16',))
            cp('dve', fl16[:], fl[:, 0:1].to_broadcast([128, 8]), r=('fl',), w=('fl16',))
            memset('dve', Cst[:], 0.0, w=('Cst',))
            memset('dve', Cb[:], 0.0, w=('Cb',))
            memset('dve', uT[:], 0.0, w=('uT',))
            si = 0
            for kc in range(8):
                for c0 in range(0, INC, 1024):
                    n = min(1024, INC - c0)
                    s = xt[si % 2]
                    dma('sp', s[:, 0:n], w_in[kc * 128:(kc + 1) * 128, c0:c0 + n], r=(), w=(s.name,), stream='L' + s.name)
                    if si % 2:
                        act(w_in_sb[:, kc, c0:c0 + n], s[:, 0:n], AF.Copy, r=(s.name, 'g1'), w=('w_in_sb1',), scale=g1[:, kc:kc + 1])
                    else:
                        ts('dve', w_in_sb[:, kc, c0:c0 + n], s[:, 0:n], g1[:, kc:kc + 1], None, ALU.mult, None,
                           r=(s.name, 'g1'), w=('w_in_sb0',))
                    si += 1
            for kc in range(8):
                s = xt[si % 2]
                dma('sp', s[:, 0:D], w_out[kc * 128:(kc + 1) * 128, :], r=(), w=(s.name,), stream='L' + s.name)
                cp('act' if si % 2 else 'dve', w_out_sb[:, kc, :], s[:, 0:D], r=(s.name,), w=('w_out_sb%d' % (si % 2),))
                si += 1

            hb = [sb("hb0", [128, D], BF16)] * 2
            hT = [sb(f"hT{i}", [128, 8, 128], BF16) for i in range(2)]
            ssq = sb("ssq", [128, 4])
            cv = sb("cv", [128, 4, 128])
            c16 = sb("c16", [128, 4, 128], BF16)
            gts = sb("gts", [128, 8])
            spl = sb("spl", [128, 8])
            gcol = sb("gcol", [128, 16])
            kk = sb("kk", [128, 4, 128], BF16)
            kT16 = sb("kT16", [128, 4, 128], BF16)
            qT16 = sb("qT16", [128, 4, 128], BF16)
            vt = sb("vt", [128, 4, 129], BF16)
            sm = sb("sm", [128, 4, 128], BF16)
            pc = sb("pc", [128, 32])
            hn = sb("hn", [128, 4, 128])
            szb = sb("szb", [128, 4, 128])
            skc = sb("skc", [128, 4, 128])
            sqa = sb("sqa", [128, 512], BF16)
            rra = sb("rra", [128, 512])
            vx = [sb(f"vx{i}", [128, 8, 65], BF16) for i in range(2)]
            vxh = [sb(f"vxh{i}", [128, 8, 65], BF16) for i in range(2)]
            for i in range(2):
                memset('pool', vx[i][:], 1.0, w=(vx[i].name,))
                cp('dve', vxh[i][:, :, 64], fl16[:], r=('fl16',), w=(vxh[i].name,))

            SC = 128.0 ** -0.5

            fronted = set()

            def tileA_front(t):
                if t in fronted or t >= NT:
                    return
                fronted.add(t)
                b = t % 2
                X, HB, HT = xt[b], hb[b], hT[b]
                dma('sp', X[:], xall[t * 128:(t + 1) * 128, :], r=(), w=(X.name,), stream='L' + X.name)
                act(HB[:], X[:], AF.Square, r=(X.name,), w=(HB.name, 'ssq0'), accum=ssq[:, 0:1])
                act(ssq[:, 1:2], ssq[:, 0:1], AF.Sqrt, r=('ssq0',), w=('ssq1',), bias=EPS, scale=1.0 / D)
                recip(ssq[:, 2:3], ssq[:, 1:2], r=('ssq1',), w=('ssq2',))
                ts('dve', HB[:], X[:], ssq[:, 2:3], None, ALU.mult, None, r=(X.name, 'ssq2'), w=(HB.name,))
                pT, kpT = nbank()
                pTb = pT[:].bitcast(BF16)
                for kc in range(8):
                    tr(pTb[:, kc * 128:(kc + 1) * 128], HB[:, kc * 128:(kc + 1) * 128], idb[:],
                       r=(HB.name, 'idb'), w=(kpT,))
                cp('act', HT[:].rearrange("p a b -> p (a b)"), pTb[:, 0:1024], r=(kpT,), w=(HT.name,))

            def tileA(t):
                tileA_front(t)
                tileA_front(t + 1)
                role = 'own' if t >= NPRE else ('halo' if t >= T0H else 'pre')
                b = t % 2
                X, HB, HT = xt[b], hb[b], hT[b]

                def fmaj(off):
                    ps, kps = nbank()
                    for fc in range(4):
                        for kc in range(8):
                            mm(ps[:, fc * 128:(fc + 1) * 128], w_in_sb[:, kc, off + fc * 128: off + (fc + 1) * 128],
                               HT[:, kc, :], kc == 0, kc == 7, r=('w_in_sb0', 'w_in_sb1', HT.name), w=(kps,))
                    return ps, kps

                def tmaj(off, n):
                    ps, kps = nbank()
                    for kc in range(8):
                        mm(ps[:, 0:n], HT[:, kc, :], w_in_sb[:, kc, off:off + n], kc == 0, kc == 7,
                           r=('w_in_sb0', 'w_in_sb1', HT.name), w=(kps,))
                    return ps, kps

                psu, kpsu = fmaj(0)
                cp('dve', uT[:, :, 0:3], uT[:, :, 128:131], r=('uT',), w=('uT',))
                cp('act', uT[:, :, 3:131], psu[:].rearrange("p (a b) -> p a b", a=4), r=(kpsu,), w=('uT',))
                for h in range(4):
                    ts('dve', cv[:, h, :], uT[:, h, 0:128], cw[:, h * 4:h * 4 + 1], cb[:, h:h + 1], ALU.mult, ALU.add,
                       r=('uT', 'cw_sb', 'cb_sb'), w=('cv',))
                    for j in range(1, 4):
                        stt(cv[:, h, :], uT[:, h, j:j + 128], cw[:, h * 4 + j:h * 4 + j + 1], cv[:, h, :],
                            ALU.mult, ALU.add, r=('uT', 'cv', 'cw_sb'), w=('cv',))
                act(c16[:], cv[:], AF.Silu, r=('cv',), w=('c16',))
                if role == 'own':
                    act(skc[:], cv[:], AF.Silu, r=('cv',), w=('skc',))
                if role in ('halo', 'own'):
                    A = (t - T0H) * 128
                    pos = ((A // 2048) % 2) * 2048 + A % 2048

                    def qknorm(off, gt, gname, dst, q):
                        ps, kps = fmaj(off)
                        act(sqa[:], ps[:], AF.Square, r=(kps,), w=('sqa',))
                        pss, kpss = nbank()
                        mm(pss[:], blk16[:], sqa[:], True, True, r=('blk16', 'sqa'), w=(kpss,))
                        if q:
                            act(rra[:], pss[:], AF.Sqrt, r=(kpss,), w=('rra',), bias=64.0 * EPS, scale=1.0)
                        else:
                            act(rra[:], pss[:], AF.Sqrt, r=(kpss,), w=('rra',), bias=EPS, scale=1.0 / 64)
                        recip(rra[:], rra[:], r=('rra',), w=('rra',))
                        for g in range(4):
                            stt(dst(g), ps[:, g * 128:(g + 1) * 128], gt[:, g:g + 1], rra[:, g * 128:(g + 1) * 128],
                                ALU.mult, ALU.mult, r=(kps, gname, 'rra'), w=(dst.key,))

                    def dk(g):
                        return kT_all[:, g, pos:pos + 128]
                    dk.key = 'kT_all'
                    qknorm(2056, kg, 'kg_sb', dk, False)
                    if role == 'own':
                        tl = (t - NPRE) % 16

                        def dq(g):
                            return qT_mb[:, g, tl * 128:(tl + 1) * 128]
                        dq.key = 'qT_mb'
                        qknorm(1544, qg, 'qg_sb', dq, True)
                    psv, kpsv = tmaj(2568, 512)
                    VX = (vx if role == 'own' else vxh)[b]
                    cp('act', VX[:, :, 0:64], psv[:].rearrange("p (a b) -> p a b", a=8), r=(kpsv,), w=(VX.name,))
                    dma('sp', vscr[A:A + 128, :], VX[:].rearrange("p a b -> p (a b)"), r=(VX.name,), w=('vscr',),
                        stream='S' + VX.name)
                if role == 'own':
                    psz, kpsz = fmaj(1024)
                    act(szb[:].rearrange("p a b -> p (a b)"), psz[:], AF.Sigmoid, r=(kpsz,), w=('szb',))
                psvm, kpsvm = tmaj(512, 512)
                psif, kpsif = tmaj(1536, 8)
                tt('dve', gts[:], psif[:, 0:8], gb[:], ALU.add, r=(kpsif, 'gb_sb'), w=('gts',))
                act(spl[:, 0:4], gts[:, 4:8], AF.Exp, r=('gts',), w=('spl0',), scale=-1.0)
                act(spl[:, 4:8], spl[:, 0:4], AF.Ln, r=('spl0',), w=('spl1',), bias=1.0)
                psb, kpsb = nbank()
                mm(psb[:, 0:4], tri[:], spl[:, 4:8], True, True, r=('tri', 'spl1'), w=(kpsb,))
                mm(psb[:, 4:8], one[:], spl[:, 4:8], True, True, r=('one', 'spl1'), w=(kpsb,))
                tt('dve', gcol[:, 0:4], psb[:, 0:4], gts[:, 0:4], ALU.add, r=(kpsb, 'gts'), w=('ga',))
                act(gcol[:, 4:8], gcol[:, 0:4], AF.Exp, r=('ga',), w=('gea',))
                act(gcol[:, 8:16], psb[:, 0:8], AF.Exp, r=(kpsb,), w=('geb',), scale=-1.0)
                psk, kpsk = nbank()
                for h in range(4):
                    mm(psk[:, h * 128:(h + 1) * 128], c16[:, h, :], wk16[:, h, :], True, True, r=('c16', 'wk16'), w=(kpsk,))
                for h in range(4):
                    ts('dve', kk[:, h, :], psk[:, h * 128:(h + 1) * 128], gcol[:, 12 + h:13 + h], SC, ALU.mult, ALU.mult,
                       r=(kpsk, 'geb'), w=('kk',))
                for h in range(4):
                    act(vt[:, h, 0:128], psvm[:, h * 128:(h + 1) * 128], AF.Copy, r=(kpsvm, 'gea'), w=('vt',),
                        scale=gcol[:, 4 + h:5 + h])
                cp('dve', vt[:, :, 128], gcol[:, 4:8], r=('gea',), w=('vt',))
                if role == 'own':
                    tl = (t - NPRE) % 16
                    psq, kpsq = nbank()
                    pskT, kpskT = nbank()
                    for h in range(4):
                        mm(psq[:, h * 128:(h + 1) * 128], wq16[:, h, :], c16[:, h, :], True, True, r=('c16', 'wq16'), w=(kpsq,))
                    for h in range(4):
                        mm(pskT[:, h * 128:(h + 1) * 128], wk16[:, h, :], c16[:, h, :], True, True, r=('c16', 'wk16'), w=(kpskT,))
                    cp('act', qT16[:].rearrange("p a b -> p (a b)"), psq[:], r=(kpsq,), w=('qT16',))
                    act(kT16[:].rearrange("p a b -> p (a b)"), pskT[:], AF.Copy, r=(kpskT,), w=('kT16',), scale=SC)
                    psS, kpsS = nbank()
                    for h in range(4):
                        mm(psS[:, h * 128:(h + 1) * 128], kT16[:, h, :], qT16[:, h, :], True, True, r=('kT16', 'qT16'), w=(kpsS,))
                    tt('dve', sm[:], psS[:].rearrange("p (a b) -> p a b", a=4),
                       tri[:].unsqueeze(1).to_broadcast([128, 4, 128]), ALU.mult, r=(kpsS, 'tri'), w=('sm',))
                    psN = [nbank(), nbank()]
                    for h in range(4):
                        pn, kpn = psN[h // 2]
                        o = pn[:, (h % 2) * 129:(h % 2) * 129 + 129]
                        mm(o, sm[:, h, :], vt[:, h, :], True, False, r=('sm', 'vt'), w=(kpn,))
                        mm(o, qT16[:, h, :], Cb[:, h, :], False, True, r=('qT16', 'Cb'), w=(kpn,))
                    for hp in range(2):
                        pn, kpn = psN[hp]
                        tt('dve', pc[:, 2 * hp:2 * hp + 2], pn[:, 128:258:129], gcol[:, 8 + 2 * hp:10 + 2 * hp], ALU.mult,
                           r=(kpn, 'geb'), w=('pc_tden',))
                    act(pc[:, 4:8], pc[:, 0:4], AF.Abs, r=('pc_tden',), w=('pc_t2',))
                    ts('dve', pc[:, 4:8], pc[:, 4:8], 1.0, None, ALU.max, None, r=('pc_t2',), w=('pc_t2',))
                    recip(pc[:, 8:12], pc[:, 4:8], r=('pc_t2',), w=('pc_rt',))
                    tt('dve', pc[:, 12:16], pc[:, 8:12], gcol[:, 8:12], ALU.mult, r=('pc_rt', 'geb'), w=('pc_sc',))
                    for h in range(4):
                        pn, kpn = psN[h // 2]
                        act(junk[:, 0:128], pn[:, (h % 2) * 129:(h % 2) * 129 + 128], AF.Square, r=(kpn,),
                            w=('pc_ssn',), accum=pc[:, 16 + h:17 + h])
                    tt('dve', pc[:, 20:24], pc[:, 12:16], pc[:, 12:16], ALU.mult, r=('pc_sc',), w=('pc_v',))
                    tt('dve', pc[:, 20:24], pc[:, 20:24], pc[:, 16:20], ALU.mult, r=('pc_v', 'pc_ssn'), w=('pc_v',))
                    act(pc[:, 24:28], pc[:, 20:24], AF.Sqrt, r=('pc_v',), w=('pc_sq',), bias=EPS, scale=1.0 / 128)
                    recip(pc[:, 24:28], pc[:, 24:28], r=('pc_sq',), w=('pc_sq',))
                    tt('dve', pc[:, 28:32], pc[:, 24:28], pc[:, 12:16], ALU.mult, r=('pc_sq', 'pc_sc'), w=('pc_fac',))
                    for h in range(4):
                        pn, kpn = psN[h // 2]
                        act(hn[:, h, :], pn[:, (h % 2) * 129:(h % 2) * 129 + 128], AF.Copy, r=(kpn, 'pc_fac'), w=('hn',),
                            scale=pc[:, 28 + h:29 + h])
                    psH, kpsH = nbank()
                    for h in range(4):
                        tr(psH[:, h * 128:(h + 1) * 128], hn[:, h, :], idf[:], r=('hn', 'idf'), w=(kpsH,))
                    for h in range(4):
                        ts('dve', skc[:, h, :], skc[:, h, :], sk[:, h:h + 1], None, ALU.mult, None, r=('skc', 'sk_sb'), w=('skc',))
                    for h in range(4):
                        stt(skc[:, h, :], psH[:, h * 128:(h + 1) * 128], mg[:, h:h + 1], skc[:, h, :], ALU.mult, ALU.add,
                            r=(kpsH, 'mg_sb', 'skc'), w=('skc',))
                    tt('dve', ymT_mb[:, :, tl * 128:(tl + 1) * 128], skc[:], szb[:], ALU.mult, r=('skc', 'szb'), w=('ymT_mb',))
                psP = [nbank(), nbank()]
                for h in range(4):
                    pp, kpp = psP[h // 2]
                    mm(pp[:, (h % 2) * 129:(h % 2) * 129 + 129], kk[:, h, :], vt[:, h, :], True, True, r=('kk', 'vt'), w=(kpp,))
                for h in range(4):
                    pp, kpp = psP[h // 2]
                    stt(Cst[:, h, :], Cst[:, h, :], gcol[:, 12 + h:13 + h], pp[:, (h % 2) * 129:(h % 2) * 129 + 129],
                        ALU.mult, ALU.add, r=('Cst', 'geb', kpp), w=('Cst',))
                if t == NPRE - 1:
                    ts('dve', Cst[:], Cst[:], fl[:, 0:1], None, ALU.mult, None, r=('Cst', 'fl'), w=('Cst',))
                cp('act', Cb[:], Cst[:], r=('Cst',), w=('Cb',))

            vk = [sb(f"vk{i}", [128, 8, 65], BF16) for i in range(4)]
            sbb = [sb(f"sbb{i}", [128, 4, 128]) for i in range(2)]
            pTt = [sb(f"pTt{i}", [128, 4, 128], BF16) for i in range(2)]
            osb = [sb(f"osb{i}", [128, 8, 65]) for i in range(2)]
            att_ctr = [0]

            SBs = [sbb[0], sbb[1], hn, szb]
            PTs = [pTt[0], pTt[1], sm, c16]

            def attention(m):
                A0 = 2048 * (m + 1)
                its = []
                for pi, (win, d) in enumerate(PATS):
                    span = 128 * d
                    first = True
                    for blk in range(2048 // span):
                        for rr_ in range(d):
                            i = att_ctr[0]
                            att_ctr[0] += 1
                            qa0 = A0 + blk * span + rr_
                            for hg in range(2):
                                its.append(dict(pi=pi, d=d, span=span, qa0=qa0, ka=(qa0 - span, qa0),
                                                vks=(vk[(2 * i) % 4], vk[(2 * i + 1) % 4]), ql=qa0 - A0,
                                                O=osb[i % 2], hg=hg, newpat=first and hg == 0))
                            first = False

                def stage1(n, I):
                    d, span, ka, vks, ql, hg, pi = I['d'], I['span'], I['ka'], I['vks'], I['ql'], I['hg'], I['pi']
                    if I['newpat']:
                        dma('sp', tb[:], tbd[pi], r=(), w=('tb_sb',), stream='Ltb')
                    if hg == 0:
                        for j in range(2):
                            dma('act', vks[j][:].rearrange("p a b -> p (a b)"), vscr[ka[j]:ka[j] + span - d + 1:d, :],
                                r=('vscr',), w=(vks[j].name,), stream='L' + vks[j].name)
                    for j in range(2):
                        S_, PT_ = SBs[2 * (n % 2) + j], PTs[2 * (n % 2) + j]
                        kp = ((ka[j] // 2048) % 2) * 2048 + ka[j] % 2048
                        ps, kps = nbank()
                        for hh in range(4):
                            head = 2 * hh + hg
                            g, po = head // 2, (head % 2) * 64
                            mm(ps[:, hh * 128:(hh + 1) * 128], kT_all[po:po + 64, g, kp:kp + span - d + 1:d],
                               qT_mb[po:po + 64, g, ql:ql + span - d + 1:d], True, True, r=('kT_all', 'qT_mb'), w=(kps,))
                        c0 = 128 if j == 0 else 0
                        tbv = tb[:].rearrange("p (h c) -> p h c", h=8)[:, hg:8:2, c0:c0 + 128]
                        tt('dve', S_[:], ps[:].rearrange("p (a b) -> p a b", a=4), tbv, ALU.add,
                           r=(kps, 'tb_sb'), w=(S_.name,))
                        act(PT_[:], S_[:], AF.Exp, r=(S_.name,), w=(PT_.name,))

                def stage2(n, I):
                    d, span, vks, ql, hg, pi, O = I['d'], I['span'], I['vks'], I['ql'], I['hg'], I['pi'], I['O']
                    po_, kpo = nbank()
                    for hh in range(4):
                        head = 2 * hh + hg
                        for j in range(2):
                            PT_ = PTs[2 * (n % 2) + j]
                            mm(po_[:, hh * 65:hh * 65 + 65], PT_[:, hh, :], vks[j][:, head, :], j == 0, j == 1,
                               r=(PT_.name, vks[j].name), w=(kpo,))
                    cp('act' if hg else 'dve', O[:, hg:8:2, :],
                       po_[:, 0:260].rearrange("p (a b) -> p a b", a=4), r=(kpo,), w=(O.name,))
                    if hg == 1:
                        dma('sp', oscr[pi, ql + m * 2048:ql + m * 2048 + span - d + 1:d, :], O[:].rearrange("p a b -> p (a b)"),
                            r=(O.name,), w=('oscr',), stream='S' + O.name)

                stage1(0, its[0])
                for n in range(len(its)):
                    if n + 1 < len(its):
                        stage1(n + 1, its[n + 1])
                    stage2(n, its[n])

            o3 = [[sb(f"o3_{p}", [128, 8, 65]) for p in range(3)]] * 2
            ya16 = sb("ya16", [128, 8, 64], BF16)
            rden = sb("rden", [128, 8])
            yaT = sb("yaT", [128, 4, 128], BF16)

            def combine(m, tl):
                to = m * 16 + tl
                b = tl % 2
                os_ = o3[b]
                for p in range(3):
                    dma('act', os_[p][:].rearrange("p a b -> p (a b)"), oscr[p, to * 128:(to + 1) * 128, :],
                        r=('oscr',), w=(os_[p].name,), stream='L' + os_[p].name)
                X = xt[b]
                dma('sp', X[:], xall[(NPRE + to) * 128:(NPRE + to + 1) * 128, :], r=(), w=(X.name,), stream='L' + X.name)
                tt('dve', os_[0][:], os_[0][:], os_[1][:], ALU.add, r=(os_[0].name, os_[1].name), w=(os_[0].name,))
                tt('dve', os_[0][:], os_[0][:], os_[2][:], ALU.add, r=(os_[0].name, os_[2].name), w=(os_[0].name,))
                recip(rden[:], os_[0][:, :, 64], r=(os_[0].name,), w=('rden',))
                tt('dve', ya16[:], os_[0][:, :, 0:64], rden[:].unsqueeze(2).to_broadcast([128, 8, 64]), ALU.mult,
                   r=(os_[0].name, 'rden'), w=('ya16',))
                pT, kpT = nbank()
                pTb = pT[:].bitcast(BF16)
                yaf = ya16[:].rearrange("p a b -> p (a b)")
                for fc in range(4):
                    tr(pTb[:, fc * 128:(fc + 1) * 128], yaf[:, fc * 128:(fc + 1) * 128], idb[:], r=('ya16', 'idb'), w=(kpT,))
                cp('act', yaT[:].rearrange("p a b -> p (a b)"), pTb[:, 0:512], r=(kpT,), w=('yaT',))
                X1 = X
                for nh in range(2):
                    py, kpy = nbank()
                    for kc in range(8):
                        lhs = ymT_mb[:, kc, tl * 128:(tl + 1) * 128] if kc < 4 else yaT[:, kc - 4, :]
                        mm(py[:], lhs, w_out_sb[:, kc, nh * 512:(nh + 1) * 512], kc == 0, kc == 7,
                           r=('ymT_mb', 'yaT', 'w_out_sb0', 'w_out_sb1'), w=(kpy,))
                    tt('dve', X1[:, nh * 512:(nh + 1) * 512], py[:], X[:, nh * 512:(nh + 1) * 512], ALU.add,
                       r=(kpy, X.name), w=(X.name,))
                dma('sp', outd[to * 128:(to + 1) * 128, :], X1[:], r=(X1.name,), w=('outd',), stream='S' + X1.name)

            cst = sb("cst", [128, 512])
            cst16 = sb("cst16", [128, 512], BF16)
            conv_i = [0]
            NCH = 512

            def convert(n):
                for _ in range(n):
                    i = conv_i[0]
                    if i >= NCH:
                        return
                    conv_i[0] += 1
                    rb_, cb_ = i // 4, i % 4
                    dma('pool', cst[:], exuv[rb_ * 128:(rb_ + 1) * 128, cb_ * 512:(cb_ + 1) * 512], r=(), w=('cst',), stream='Lcst')
                    cp('pool', cst16[:], cst[:], r=('cst',), w=('cst16',))
                    dma('pool', exb[rb_ * 128:(rb_ + 1) * 128, cb_ * 512:(cb_ + 1) * 512], cst16[:], r=('cst16',), w=(), stream='Scst16')
            per_tile = -(-NCH // NT)
            _tileA = tileA

            def tileA(t):
                convert(per_tile)
                _tileA(t)
            import os
            KS = os.environ.get("KSTOP", "")
            if KS == "setup":
                pass
            elif KS == "t1":
                tileA(0)
            elif KS == "pre":
                for t in range(NPRE):
                    tileA(t)
            elif KS == "own1":
                for t in range(NPRE + 1):
                    tileA(t)
            elif KS == "att":
                for t in range(NPRE + 16):
                    tileA(t)
                attention(0)
            else:
                for t in range(NPRE):
                    tileA(t)
                for m in range(NOWN // 16):
                    for tl in range(16):
                        tileA(NPRE + m * 16 + tl)
                    attention(m)
                    for tl in range(16):
                        combine(m, tl)
                convert(NCH)
            nsA = P.emit(es)
        if with_peer:
            esB = ExitStack()
            with esB:
                buildB(nc, es, esB, NOWN, outd, g2d, w_query, skeys, c_iota, c_idf, exb)
    return nc


def buildB(nc, es, esB, NOWN, outd, g2d, w_query, skeys, c_iota, c_idf, exb):
    P = Prog(nc, "B")
    NG, NDG, GS = 24, 8, 4

    def sb(name, shape, dt=F32):
        return esB.enter_context(nc.sbuf_tensor(name, list(shape), dt))

    rbanks = [esB.enter_context(nc.psum_tensor(f"rb{i}", [128, 512], F32)) for i in range(4)]
    h2f = esB.enter_context(nc.psum_tensor("h2f", [128, D], F32))
    pout = [esB.enter_context(nc.psum_tensor(f"pout{i}", [128, 512], F32)) for i in range(2)]
    bank_i = [0]

    def nbank():
        b = bank_i[0] % 4
        bank_i[0] += 1
        return rbanks[b], ('rb', b)

    def dma(q, out, in_, r, w, stream, final=False):
        P.add(q, lambda e: e.dma_start(out=out, in_=in_), r=r, w=w, stream=stream, final=final)

    def mm(out, lhsT, rhs, start, stop, r, w):
        P.add('pe', lambda e: e.matmul(out, lhsT, rhs, start=start, stop=stop), r=r, w=w)

    def tr(out, in_, ident, r, w):
        P.add('pe', lambda e: e.transpose(out, in_, ident), r=r, w=w)

    def act(out, in_, func, r, w, bias=None, scale=None, accum=None):
        kw = {}
        if bias is not None:
            kw['bias'] = bias
        if scale is not None:
            kw['scale'] = scale
        if accum is not None:
            kw['accum_out'] = accum
        P.add('act', lambda e: e.activation(out, in_, func, **kw), r=r, w=w)

    def tt(eng, out, in0, in1, op, r, w):
        P.add(eng, lambda e: e.tensor_tensor(out, in0, in1, op), r=r, w=w)

    def ts(eng, out, in0, s1, s2, op0, op1, r, w):
        if op1 is None:
            P.add(eng, lambda e: e.tensor_scalar(out, in0, s1, None, op0), r=r, w=w)
        else:
            P.add(eng, lambda e: e.tensor_scalar(out, in0, s1, s2, op0, op1), r=r, w=w)

    def stt(out, in0, scalar, in1, op0, op1, r, w, accum=None):
        P.add('dve', lambda e: e.scalar_tensor_tensor(out, in0, scalar, in1, op0, op1, accum_out=accum), r=r, w=w)

    def cp(eng, out, in_, r, w):
        if eng == 'act':
            P.add('act', lambda e: e.copy(out, in_), r=r, w=w)
        else:
            P.add(eng, lambda e: e.tensor_copy(out, in_), r=r, w=w)

    def recip(out, in_, r, w):
        P.add('dve', lambda e: e.reciprocal(out, in_), r=r, w=w)

    def red(out, in_, op, r, w):
        P.add('dve', lambda e: e.tensor_reduce(out, in_, AX.X, op), r=r, w=w)

    wq_sb = sb("wq_sb", [128, 8, 2048], BF16)
    skT = sb("skT", [128, 16, 128], BF16)
    idf = sb("idfB", [128, 128])
    idb = sb("idbB", [128, 128], BF16)
    g2b = sb("g2b", [128, D])
    iot = sb("iot", [128, 16])
    x1t = [sb(f"x1t{i}", [128, D]) for i in range(3)]
    hb2 = sb("hb2", [128, D], BF16)
    h2T = sb("h2T", [128, 8, 128], BF16)
    junk = sb("junkB", [128, D], BF16)
    junkD = sb("junkD", [128, D], BF16)
    ssq = [sb(f"ssqB{i}", [128, 4]) for i in range(2)]
    qpT = sb("qpT", [128, 16, 128], BF16)
    sc = sb("sc", [128, 16, 128])
    sc2 = sb("sc2", [128, 16, 128])
    v16 = sb("v16", [128, 16, 16])
    i16 = sb("i16", [128, 16, 16], U32)
    i16f = sb("i16f", [128, 16, 16])
    cand = sb("cand", [128, 8, 256])
    cand2 = sc[:].rearrange("p (h a) k -> p h (a k)", h=8)
    vc = sb("vc", [128, 8, 16])
    ic = sb("ic", [128, 8, 16], U32)
    ich = sb("ich", [128, 8, 16], U32)
    icl = sb("icl", [128, 8, 16], U32)
    ichf = sb("ichf", [128, 8, 16])
    iclf = sb("iclf", [128, 8, 16])
    eq = sc2[:].rearrange("p (h a) (b c) -> p h (a b) c", h=8, c=16)
    e12 = sb("e12", [128, 2, 8, 16])
    idxf = sb("idxf", [128, 128])
    idx = [sb(f"idx{i}", [128, 128], I32) for i in range(2)]
    gate = [sb(f"gate{i}", [128, 128]) for i in range(2)]
    gtmp = sb("gtmp", [128, 8, 16])
    gs = sb("gs", [128, 16])
    pre = sb("pre", [128, 128])
    av = sb("av", [128, 128])
    dg = [sb(f"dg{i}", [128, 128], BF16) for i in range(NDG)]
    gbuf = [sb(f"gb{i}", [128, 2 * D], BF16) for i in range(NG)]
    osb = [sb(f"osbB{i}", [128, D]) for i in range(2)]

    SU = 'setupB'
    dma('sp', idf[:], c_idf, r=(), w=('idfB',), stream=SU, final=True)
    dma('sp', g2b[:], g2d.partition_broadcast(128), r=(), w=('g2b',), stream=SU, final=True)
    dma('sp', iot[:], c_iota.partition_broadcast(128), r=(), w=('iot',), stream=SU, final=True)
    cp('dve', idb[:], idf[:], r=('idfB',), w=('idbB',))
    si = 0
    for kc in range(8):
        for c0 in range(0, 2048, 1024):
            s_ = x1t[si % 2]
            dma('sp', s_[:], w_query[kc * 128:(kc + 1) * 128, c0:c0 + 1024], r=(), w=(s_.name,), stream='L' + s_.name)
            cp('act' if si % 2 else 'dve', wq_sb[:, kc, c0:c0 + 1024], s_[:], r=(s_.name,), w=('wq_sb%d' % (si % 2),))
            si += 1
    for q4 in range(4):
        s_ = x1t[si % 2]
        dma('sp', s_[:, 0:512].rearrange("p (a b) -> p a b", a=4), skeys[q4 * 4:(q4 + 1) * 4].rearrange("a k c -> k a c"),
            r=(), w=(s_.name,), stream='L' + s_.name)
        ps, kps = nbank()
        for a in range(4):
            tr(ps[:, a * 128:(a + 1) * 128], s_[:, a * 128:(a + 1) * 128], idf[:], r=(s_.name, 'idfB'), w=(kps,))
        cp('act', skT[:, q4 * 4:(q4 + 1) * 4, :].rearrange("p a b -> p (a b)"), ps[:], r=(kps,), w=('skT',))
        si += 1

    def FE(t):
        X = x1t[t % 3]
        SS = ssq[t % 2]
        IDX = idx[t % 2]
        G = gate[t % 2]
        dma('sp', X[:], outd[t * 128:(t + 1) * 128, :], r=(('outd', t),), w=(X.name,), stream='L' + X.name)
        act(junk[:], X[:], AF.Square, r=(X.name,), w=(SS.name + 'a',), accum=SS[:, 0:1])
        act(SS[:, 1:2], SS[:, 0:1], AF.Sqrt, r=(SS.name + 'a',), w=(SS.name + 'b',), bias=EPS, scale=1.0 / D)
        yield
        recip(SS[:, 2:3], SS[:, 1:2], r=(SS.name + 'b',), w=(SS.name,))
        stt(hb2[:], X[:], SS[:, 2:3], g2b[:], ALU.mult, ALU.mult, r=(X.name, SS.name, 'g2b'), w=('hb2',))
        pT, kpT = nbank()
        pTb = pT[:].bitcast(BF16)
        for kc in range(8):
            tr(pTb[:, kc * 128:(kc + 1) * 128], hb2[:, kc * 128:(kc + 1) * 128], idb[:], r=('hb2', 'idbB'), w=(kpT,))
        cp('act', h2T[:].rearrange("p a b -> p (a b)"), pTb[:, 0:1024], r=(kpT,), w=('h2T',))
        yield
        for q4 in range(4):
            ps, kps = nbank()
            for a in range(4):
                blk = q4 * 4 + a
                for kc in range(8):
                    mm(ps[:, a * 128:(a + 1) * 128], wq_sb[:, kc, blk * 128:(blk + 1) * 128], h2T[:, kc, :], kc == 0, kc == 7,
                       r=('wq_sb0', 'wq_sb1', 'h2T'), w=(kps,))
            cp('act', qpT[:, q4 * 4:(q4 + 1) * 4, :].rearrange("p a b -> p (a b)"), ps[:], r=(kps,), w=('qpT',))
            yield
        for q4 in range(4):
            ps, kps = nbank()
            for a in range(4):
                blk = q4 * 4 + a
                mm(ps[:, a * 128:(a + 1) * 128], qpT[:, blk, :], skT[:, blk, :], True, True, r=('qpT', 'skT'), w=(kps,))
            cp('act', sc[:, q4 * 4:(q4 + 1) * 4, :].rearrange("p a b -> p (a b)"), ps[:], r=(kps,), w=('sc',))
            yield

        def top16_ops(src, src2, vout, iout, key, key2, tag):
            kv, ki, k2 = (key + 'v', tag), (key + 'i', tag), (key2, tag)
            return [
                (lambda e: e.max(vout[:, 0:8], src), (key,), (kv,)),
                (lambda e: e.max_index(iout[:, 0:8], vout[:, 0:8], src), (key, kv), (ki,)),
                (lambda e: e.match_replace(src2, vout[:, 0:8], src, -1e30), (key, kv), (k2,)),
                (lambda e: e.max(vout[:, 8:16], src2), (k2,), (kv,)),
                (lambda e: e.max_index(iout[:, 8:16], vout[:, 8:16], src2), (k2, kv), (ki,)),
            ]

        def emit_interleaved(lists):
            for grp_ops in zip(*lists):
                for fn, r_, w_ in grp_ops:
                    P.add('dve', fn, r=r_, w=w_)
                yield

        for b0 in range(0, 16, 8):
            yield from emit_interleaved([top16_ops(sc[:, blk, :], sc2[:, blk, :], v16[:, blk, :], i16[:, blk, :], 'sc', 'sc2', blk)
                                         for blk in range(b0, b0 + 8)])
        SCV = tuple(('scv', blk) for blk in range(16))
        SCI = tuple(('sci', blk) for blk in range(16))
        SC2 = tuple(('sc2', blk) for blk in range(16))
        EQK = ('sc2',) + SC2
        cp('dve', i16f[:], i16[:], r=SCI, w=('i16f',))
        v4 = v16[:].rearrange("p (h two) s -> p h two s", two=2)
        tt('dve', cand[:].rearrange("p h (i j) -> p h i j", i=16),
           v4[:, :, 0, :].unsqueeze(3).to_broadcast([128, 8, 16, 16]),
           v4[:, :, 1, :].unsqueeze(2).to_broadcast([128, 8, 16, 16]), ALU.add, r=SCV, w=('cand',))
        for h0 in range(0, 8, 8):
            lists = [top16_ops(cand[:, h, :], cand2[:, h, :], vc[:, h, :], ic[:, h, :], 'cand', 'scA', h) for h in range(h0, h0 + 8)]
            for ops in lists:
                fn, r_, w_ = ops[2]
                ops[2] = (fn, r_, w_ + ('sc',))
                for q in (3, 4):
                    fn, r_, w_ = ops[q]
                    ops[q] = (fn, r_ + ('sc',), w_)
            yield from emit_interleaved(lists)
        CV = tuple(('candv', h) for h in range(8))
        CI = tuple(('candi', h) for h in range(8))
        P.add('dve', lambda e: e.tensor_single_scalar(ich[:], ic[:], 4, ALU.logical_shift_right), r=CI, w=('ich',))
        P.add('dve', lambda e: e.tensor_single_scalar(icl[:], ic[:], 15, ALU.bitwise_and), r=CI, w=('icl',))
        cp('dve', ichf[:], ich[:], r=('ich',), w=('ichf',))
        cp('dve', iclf[:], icl[:], r=('icl',), w=('iclf',))
        i4 = i16f[:].rearrange("p (h two) s -> p h two s", two=2)
        iob = iot[:].unsqueeze(1).unsqueeze(1).to_broadcast([128, 8, 16, 16])
        for w_, srcf in ((0, ichf), (1, iclf)):
            tt('dve', eq, srcf[:].unsqueeze(3).to_broadcast([128, 8, 16, 16]), iob, ALU.is_equal,
               r=(srcf.name, 'iot'), w=EQK)
            tt('dve', eq, eq, i4[:, :, w_, :].unsqueeze(2).to_broadcast([128, 8, 16, 16]), ALU.mult,
               r=EQK + ('i16f',), w=EQK)
            red(e12[:, w_, :, :], eq, ALU.add, r=EQK, w=('e12',))
            yield
        stt(idxf[:], e12[:, 0, :, :].rearrange("p a b -> p (a b)"), 128.0, e12[:, 1, :, :].rearrange("p a b -> p (a b)"),
            ALU.mult, ALU.add, r=('e12',), w=('idxf',))
        cp('dve', IDX[:], idxf[:], r=('idxf',), w=(IDX.name,))
        tt('dve', gtmp[:], vc[:], vc[:, :, 0:1].to_broadcast([128, 8, 16]), ALU.subtract, r=CV, w=('gtmp',))
        act(gtmp[:], gtmp[:], AF.Exp, r=('gtmp',), w=('gtmp',))
        yield
        red(gs[:, 0:8], gtmp[:], ALU.add, r=('gtmp',), w=('gs',))
        recip(gs[:, 8:16], gs[:, 0:8], r=('gs',), w=('gs',))
        tt('dve', G[:].rearrange("p (a b) -> p a b", a=8), gtmp[:], gs[:, 8:16].unsqueeze(2).to_broadcast([128, 8, 16]),
           ALU.mult, r=('gtmp', 'gs'), w=(G.name,))

    gctr = [0]

    def BE(t, fe=None):
        X = x1t[t % 3]
        SS = ssq[t % 2]
        IDX = idx[t % 2]
        G = gate[t % 2]
        stt(h2f[:], X[:], SS[:, 2:3], g2b[:], ALU.mult, ALU.mult, r=(X.name, SS.name, 'g2b'), w=('h2f',))
        ngrp = 128 // GS

        def front(grp):
            bufs = []
            for s_ in range(GS):
                k = grp * GS + s_
                c = gctr[0]
                gctr[0] += 1
                B_ = gbuf[c % NG]
                bufs.append(B_)
                P.add('pool', lambda e, B_=B_, k=k: e.indirect_dma_start(
                    out=B_[:, :], out_offset=None, in_=exb[:, :],
                    in_offset=bass.IndirectOffsetOnAxis(ap=IDX[:, k:k + 1], axis=0)),
                    r=(IDX.name,), w=(B_.name,), stream='G' + B_.name)
                stt(junkD[:], B_[:, 0:D], 1.0, h2f[:], ALU.mult, ALU.mult, r=(B_.name, 'h2f'), w=(('pre', grp, s_),),
                    accum=pre[:, k:k + 1])
            return bufs

        def back(grp, bufs):
            g0 = grp * GS
            act(av[:, g0:g0 + GS], pre[:, g0:g0 + GS], AF.Gelu, r=tuple(('pre', grp, q) for q in range(GS)), w=(('av', grp),))
            tt('dve', av[:, g0:g0 + GS], av[:, g0:g0 + GS], G[:, g0:g0 + GS], ALU.mult, r=(('av', grp), G.name), w=(('av', grp),))
            for s_ in range(GS):
                k = g0 + s_
                B_ = bufs[s_]
                DG = dg[k % NDG]
                act(DG[:], idf[:], AF.Copy, r=('idfB', ('av', grp)), w=(DG.name,), scale=av[:, k:k + 1])
                for nh in range(2):
                    mm(pout[nh][:], DG[:], B_[:, D + nh * 512:D + (nh + 1) * 512], k == 0, k == 127,
                       r=(DG.name, B_.name), w=('pout%d' % nh,))

        pend = []
        for grp in range(ngrp):
            pend.append((grp, front(grp)))
            if grp == 1 and t > 0:
                finish(t - 1)
            if fe is not None:
                for _ in range(2):
                    next(fe, None)
            if len(pend) > 2:
                back(*pend.pop(0))
        for p_ in pend:
            back(*p_)
        if fe is not None:
            for _ in fe:
                pass
        if t == NOWN - 1:
            finish(t)

    def finish(t):
        X = x1t[t % 3]
        O = osb[t % 2]
        for nh in range(2):
            tt('dve', O[:, nh * 512:(nh + 1) * 512], pout[nh][:], X[:, nh * 512:(nh + 1) * 512], ALU.add,
               r=('pout%d' % nh, X.name), w=(O.name,))
        dma('sp', outd[t * 128:(t + 1) * 128, :], O[:], r=(O.name,), w=(('outd', t),), stream='S' + O.name)

    for _ in FE(0):
        pass
    for t in range(NOWN):
        BE(t, FE(t + 1) if t + 1 < NOWN else None)
    P.emit(es)


def _bucket_table(rel_bias):
    nb, maxd = 32, 2048
    me = nb // 2
    out = np.full((3, 128, 8, 256), NEG, np.float32)
    j = np.arange(128)[:, None]
    c = np.arange(256)[None, :]
    steps = c - j
    valid = (steps >= 0) & (steps <= 128)
    for pi, (win, d) in enumerate(PATS):
        dist = np.maximum(steps, 0) * d
        nf = np.maximum(dist, 1).astype(np.float32)
        large = me + (np.log(nf / me) / np.log(maxd / me) * (nb - me)).astype(np.int32)
        large = np.minimum(large, nb - 1)
        bidx = np.where(dist < me, dist, large)
        for h in range(8):
            out[pi, :, h, :] = np.where(valid, rel_bias[bidx, h], NEG)
    return out.reshape(3, 128, 8 * 256)


def make_in_maps(inp, cores, NPRE, NOWN):
    f = lambda a: np.ascontiguousarray(a, dtype=np.float32)
    ii = np.arange(128)
    common = {
        "w_in": f(inp['w_in'][0]),
        "g1c": f(inp['norm1_g'][0].reshape(8, 128).T),
        "cw": f(inp['conv_w'][0].reshape(4, 4, 128).transpose(2, 1, 0).reshape(128, 16)),
        "cb": f(inp['conv_b'][0].reshape(4, 128).T),
        "wqm": f(inp['wq_m'][0].transpose(1, 0, 2)),
        "wkm": f(inp['wk_m'][0].transpose(1, 0, 2)),
        "gb": f(np.concatenate([inp['ig_b'][0], inp['fg_b'][0]])[None, :]),
        "mg": f(inp['mh_norm_g'][0].T),
        "sk": f(inp['skip_m'][0].T),
        "qg": f(inp['qn_g'][0].reshape(4, 128).T),
        "kg": f(inp['kn_g'][0].reshape(4, 128).T),
        "tb": _bucket_table(np.asarray(inp['rel_bias'], np.float32)),
        "w_out": f(inp['w_out'][0]),
        "c_idf": f(np.eye(128)),
        "c_tri": f(ii[:, None] <= ii[None, :]),
        "c_one": f(np.ones((128, 128))),
        "c_blk": f((ii[:, None] // 64) == (ii[None, :] // 64)),
        "g2": f(inp['norm2_g'][0][None, :]),
        "w_query": f(inp['w_query'][0]),
        "skeys": f(np.stack([inp['sub_keys1'][0], inp['sub_keys2'][0]], axis=1).reshape(16, 128, 128)),
        "c_iota": f(np.arange(16)[None, :]),
        "exuv": f(np.concatenate([inp['expert_u'][0], inp['expert_v'][0]], axis=1)),
    }
    maps = []
    for xp, xo, flag in cores:
        m = dict(common)
        m["xall"] = f(np.concatenate([xp, xo], axis=0))
        m["flg"] = np.full((128, 2), float(flag), np.float32)
        maps.append(m)
    return maps


def kernel(**inp):
    inp = {k: np.asarray(v) for k, v in inp.items()}
    x = inp['x']
    B, S, _ = x.shape
    NPRE, NOWN = 32, 32
    half = NOWN * 128
    cores = []
    for b in range(B):
        cores.append((np.zeros((NPRE * 128, D), np.float32), x[b, 0:half], 0.0))
        cores.append((x[b, 0:half], x[b, half:2 * half], 1.0))
    nc = build(NPRE, NOWN, with_peer=True)
    in_maps = make_in_maps(inp, cores, NPRE, NOWN)
    res = run_bass_kernel_spmd(nc, in_maps, core_ids=list(range(len(cores))))
    out = np.zeros((B, S, D), np.float32)
    for c in range(len(cores)):
        b, hf = c // 2, c % 2
        out[b, hf * half:(hf + 1) * half] = res.results[c]["out"]
    return out
```
